# Optimizing a Trainium2 kernel written in Bass

```python
import math
import jax, jax.numpy as jnp
from jax import lax
import numpy as np

D_MODEL = 2048
BATCH = 8
SEQ = 2048
DEPTH = 2

CHUNK = 64
N_MIXERS = 2
Q_BLOCK = 128
SB_HEADS = 16
SB_HEAD_DIM = D_MODEL // SB_HEADS
SSD_EXPAND = 2
SSD_D_INNER = SSD_EXPAND * D_MODEL
SSD_HEAD_DIM = 64
SSD_HEADS = SSD_D_INNER // SSD_HEAD_DIM
SSD_GROUPS = 8
SSD_HEADS_PER_GROUP = SSD_HEADS // SSD_GROUPS
SSD_STATE = 128
SSD_CONV = 4
SSD_CONV_DIM = SSD_D_INNER + 2 * SSD_GROUPS * SSD_STATE
SSD_IN_DIM = SSD_D_INNER + SSD_CONV_DIM + SSD_HEADS
D_FF = 5632
N_SB_LAYERS = (DEPTH + 1) // 2
N_SSD_LAYERS = DEPTH // 2
DEEPNORM_ALPHA = (2.0 * DEPTH) ** 0.25
DEEPNORM_BETA = (8.0 * DEPTH) ** -0.25
LN_EPS = 1e-5
RMS_EPS = 1e-5

kernel_name = "hybrid_stickbreak_ssd_macaron_deepnorm"


def layer_norm(x, g, b):
    xf = x.astype(jnp.float32)
    mu = jnp.mean(xf, axis=-1, keepdims=True)
    var = jnp.mean(jnp.square(xf - mu), axis=-1, keepdims=True)
    return ((xf - mu) * lax.rsqrt(var + LN_EPS) * g + b).astype(x.dtype)


def swiglu(x, w_gate_up, w_down):
    gate, up = jnp.split(x @ w_gate_up, 2, axis=-1)
    return (jax.nn.silu(gate) * up) @ w_down


def stick_breaking_attention(x, w_in, w_out):
    b, s, _ = x.shape
    qkv = (x @ w_in).reshape(b, s, 3, SB_HEADS, SB_HEAD_DIM)
    q = jnp.moveaxis(qkv[:, :, 0], 1, 2)
    k = jnp.moveaxis(qkv[:, :, 1], 1, 2)
    v = jnp.moveaxis(qkv[:, :, 2], 1, 2)
    n_blk = s // Q_BLOCK
    q_blocks = jnp.moveaxis(q.reshape(b, SB_HEADS, n_blk, Q_BLOCK, SB_HEAD_DIM), 2, 0)
    k_pos = jnp.arange(s)
    scale = SB_HEAD_DIM ** -0.5

    def one_block(args):
        blk, qb = args
        z = jnp.einsum('bhqd,bhkd->bhqk', qb, k).astype(jnp.float32) * scale
        q_pos = blk * Q_BLOCK + jnp.arange(Q_BLOCK)
        strict = k_pos[None, :] < q_pos[:, None]
        log_keep = jnp.where(strict, jax.nn.log_sigmoid(-z), 0.0)
        log_rest = lax.cumsum(log_keep, axis=3, reverse=True) - log_keep
        att = jnp.where(strict, jnp.exp(jax.nn.log_sigmoid(z) + log_rest), 0.0)
        return jnp.einsum('bhqk,bhkd->bhqd', att.astype(v.dtype), v)

    o = lax.map(one_block, (jnp.arange(n_blk), q_blocks))
    o = jnp.moveaxis(o, 0, 2).reshape(b, SB_HEADS, s, SB_HEAD_DIM)
    o = jnp.moveaxis(o, 1, 2).reshape(b, s, D_MODEL)
    return o @ w_out


def ssd_mixer(x, w_in, conv_w, conv_b, dt_bias, a_log, d_skip, norm_g, w_out):
    f32 = jnp.float32
    b, s, _ = x.shape
    nc = s // CHUNK
    G, R, P, N = SSD_GROUPS, SSD_HEADS_PER_GROUP, SSD_HEAD_DIM, SSD_STATE
    proj = x @ w_in
    z, xbc, dt_raw = jnp.split(proj, [SSD_D_INNER, SSD_D_INNER + SSD_CONV_DIM], axis=-1)
    xbc = lax.conv_general_dilated(
        xbc, conv_w[:, None, :], window_strides=(1,), padding=[(SSD_CONV - 1, 0)],
        dimension_numbers=('NWC', 'WIO', 'NWC'), feature_group_count=SSD_CONV_DIM) + conv_b
    xbc = jax.nn.silu(xbc)
    xs, bm, cm = jnp.split(xbc, [SSD_D_INNER, SSD_D_INNER + G * N], axis=-1)
    xs = xs.astype(f32).reshape(b, nc, CHUNK, G, R, P)
    bm = bm.astype(f32).reshape(b, nc, CHUNK, G, N)
    cm = cm.astype(f32).reshape(b, nc, CHUNK, G, N)
    dt = jax.nn.softplus(dt_raw.astype(f32) + dt_bias).reshape(b, nc, CHUNK, G, R)
    a = -jnp.exp(a_log.astype(f32)).reshape(G, R)
    a_cum = jnp.cumsum(dt * a, axis=2)

    causal = jnp.tril(jnp.ones((CHUNK, CHUNK), dtype=bool))[:, :, None, None]
    seg = a_cum[:, :, :, None] - a_cum[:, :, None, :]
    decay = jnp.exp(jnp.where(causal, seg, -jnp.inf))
    cb = jnp.einsum('bcign,bcjgn->bcijg', cm, bm)
    y_diag = jnp.einsum('bcijgr,bcjgrp->bcigrp', cb[..., None] * decay * dt[:, :, None], xs)

    decay_to_end = jnp.exp(a_cum[:, :, -1:] - a_cum)
    states = jnp.einsum('bcjgn,bcjgr,bcjgrp->bcgrpn', bm, decay_to_end * dt, xs)
    chunk_decay = jnp.exp(a_cum[:, :, -1])

    def step(h, inp):
        dec, st = inp
        return dec[..., None, None] * h + st, h

    h0 = jnp.zeros((b, G, R, P, N), f32)
    _, prev = lax.scan(step, h0, (jnp.moveaxis(chunk_decay, 1, 0), jnp.moveaxis(states, 1, 0)))
    prev = jnp.moveaxis(prev, 0, 1)
    y_off = jnp.einsum('bcign,bcgrpn,bcigr->bcigrp', cm, prev, jnp.exp(a_cum))

    y = y_diag + y_off + xs * d_skip.astype(f32).reshape(G, R)[:, :, None]
    y = y.reshape(b, s, SSD_D_INNER) * jax.nn.silu(z.astype(f32))
    yg = y.reshape(b, s, G, SSD_D_INNER // G)
    yg = yg * lax.rsqrt(jnp.mean(jnp.square(yg), axis=-1, keepdims=True) + RMS_EPS)
    y = yg.reshape(b, s, SSD_D_INNER) * norm_g
    return y.astype(x.dtype) @ w_out


def setup_inputs(seed: int = 0) -> dict:
    key = jax.random.key(seed)
    ks = jax.random.split(key, 20)
    f32 = jnp.float32

    def nrm(k, shape, fan_in, gain=1.0):
        return jax.random.normal(k, shape, f32) * (gain * fan_in ** -0.5)

    x = jax.random.normal(ks[0], (BATCH, SEQ, D_MODEL), f32)
    ffn_w_gate_up = nrm(ks[1], (DEPTH, 2, D_MODEL, 2 * D_FF), D_MODEL, DEEPNORM_BETA)
    ffn_w_down = nrm(ks[2], (DEPTH, 2, D_FF, D_MODEL), D_FF, DEEPNORM_BETA)
    ln_g = 1.0 + 0.02 * jax.random.normal(ks[3], (DEPTH, 3, D_MODEL), f32)
    ln_b = 0.02 * jax.random.normal(ks[4], (DEPTH, 3, D_MODEL), f32)
    sb_cols = jnp.arange(3 * D_MODEL)
    sb_scale = jnp.where(sb_cols >= 2 * D_MODEL, DEEPNORM_BETA, 1.0).astype(f32)
    sb_w_in = nrm(ks[5], (N_SB_LAYERS, D_MODEL, 3 * D_MODEL), D_MODEL) * sb_scale
    sb_w_out = nrm(ks[6], (N_SB_LAYERS, D_MODEL, D_MODEL), D_MODEL, DEEPNORM_BETA)
    ssd_cols = jnp.arange(SSD_IN_DIM)
    ssd_scale = jnp.where((ssd_cols >= SSD_D_INNER) & (ssd_cols < 2 * SSD_D_INNER),
                          DEEPNORM_BETA, 1.0).astype(f32)
    ssd_w_in = nrm(ks[7], (N_SSD_LAYERS, D_MODEL, SSD_IN_DIM), D_MODEL) * ssd_scale
    ssd_conv_w = nrm(ks[8], (N_SSD_LAYERS, SSD_CONV, SSD_CONV_DIM), SSD_CONV)
    ssd_conv_b = 0.02 * jax.random.normal(ks[9], (N_SSD_LAYERS, SSD_CONV_DIM), f32)
    dt0 = jnp.exp(jax.random.uniform(ks[10], (N_SSD_LAYERS, SSD_HEADS), f32,
                                     math.log(1e-3), math.log(1e-1)))
    ssd_dt_bias = dt0 + jnp.log(-jnp.expm1(-dt0))
    ssd_a_log = jnp.log(jax.random.uniform(ks[11], (N_SSD_LAYERS, SSD_HEADS), f32, 1.0, 16.0))
    ssd_d = 1.0 + 0.02 * jax.random.normal(ks[12], (N_SSD_LAYERS, SSD_HEADS), f32)
    ssd_norm_g = 1.0 + 0.02 * jax.random.normal(ks[13], (N_SSD_LAYERS, SSD_D_INNER), f32)
    ssd_w_out = nrm(ks[14], (N_SSD_LAYERS, SSD_D_INNER, D_MODEL), SSD_D_INNER, DEEPNORM_BETA)
    return {"x": x, "ffn_w_gate_up": ffn_w_gate_up, "ffn_w_down": ffn_w_down,
            "ln_g": ln_g, "ln_b": ln_b, "sb_w_in": sb_w_in, "sb_w_out": sb_w_out,
            "ssd_w_in": ssd_w_in, "ssd_conv_w": ssd_conv_w, "ssd_conv_b": ssd_conv_b,
            "ssd_dt_bias": ssd_dt_bias, "ssd_a_log": ssd_a_log, "ssd_d": ssd_d,
            "ssd_norm_g": ssd_norm_g, "ssd_w_out": ssd_w_out}


def reference(x, ffn_w_gate_up, ffn_w_down, ln_g, ln_b, sb_w_in, sb_w_out,
              ssd_w_in, ssd_conv_w, ssd_conv_b, ssd_dt_bias, ssd_a_log, ssd_d,
              ssd_norm_g, ssd_w_out):
    for i in range(DEPTH):
        mixer = i % N_MIXERS
        j = i // N_MIXERS
        x = layer_norm(DEEPNORM_ALPHA * x + 0.5 * swiglu(x, ffn_w_gate_up[i, 0], ffn_w_down[i, 0]),
                       ln_g[i, 0], ln_b[i, 0])
        if mixer == 0:
            mix = stick_breaking_attention(x, sb_w_in[j], sb_w_out[j])
        else:
            mix = ssd_mixer(x, ssd_w_in[j], ssd_conv_w[j], ssd_conv_b[j], ssd_dt_bias[j],
                            ssd_a_log[j], ssd_d[j], ssd_norm_g[j], ssd_w_out[j])
        x = layer_norm(DEEPNORM_ALPHA * x + mix, ln_g[i, 1], ln_b[i, 1])
        x = layer_norm(DEEPNORM_ALPHA * x + 0.5 * swiglu(x, ffn_w_gate_up[i, 1], ffn_w_down[i, 1]),
                       ln_g[i, 2], ln_b[i, 2])
    return x
```

```python
import math
import numpy as np
import concourse.bass as bass
import concourse.mybir as mybir
from concourse.bass_utils import run_bass_kernel_spmd

F32 = mybir.dt.float32
BF16 = mybir.dt.bfloat16
AF = mybir.ActivationFunctionType
ALU = mybir.AluOpType
AX = mybir.AxisListType

S = 2048
D = 2048
DFF = 5632
DEPTH = 2
NH = 16
HD = 128
TT = S // 128
KC = D // 128
ALPHA = (2.0 * DEPTH) ** 0.25
LN_EPS = 1e-5
RMS_EPS = 1e-5
SSD_DI = 4096
SSD_G = 8
SSD_N = 128
SSD_H = 64
SSD_P = 64
SSD_CONV = SSD_DI + 2 * SSD_G * SSD_N
SSD_IN = SSD_DI + SSD_CONV + SSD_H
CH = 64

ENGS = ("pe", "act", "dve", "pool", "sp")


class Op:
    __slots__ = ("eng", "fn", "deps", "dma", "signal", "event", "waits", "idx")


class Prog:
    def __init__(self, same_engine_sync=True):
        self.nc = bass.Bass("TRN2", target_bir_lowering=False)
        self.same_engine_sync = same_engine_sync
        self._sem_ctx = []
        self.eng_sem = {e: self._mksem("s_" + e) for e in ("pe", "act", "dve", "pool")}
        self.chan_sem = {}
        self.chan_cnt = {}
        self.chan_eng = {}
        self.eng_cnt = {e: 0 for e in ENGS}
        self.seen = {e: {} for e in ENGS}
        self.free_chan = {e: [] for e in ENGS}
        self._uid = 0
        self._reset_phase()
        self.n_instr = 0

    def _mksem(self, name):
        g = self.nc.semaphore(name)
        s = g.__enter__()
        self._sem_ctx.append(g)
        return s

    def _reset_phase(self):
        self.ops = []
        self.lastw = {}
        self.readers = {}
        self._ctx = []
        self.phase_chan = {}

    def sbuf(self, name, shape, dtype):
        self._uid += 1
        g = self.nc.sbuf_tensor("%s_%d" % (name, self._uid), list(shape), dtype)
        t = g.__enter__()
        self._ctx.append(g)
        return t

    def psum(self, name, shape, dtype):
        self._uid += 1
        g = self.nc.psum_tensor("%s_%d" % (name, self._uid), list(shape), dtype)
        t = g.__enter__()
        self._ctx.append(g)
        return t

    def dram(self, name, shape, dtype, kind="Internal"):
        return self.nc.dram_tensor(name, list(shape), dtype, kind=kind)

    def op(self, eng, fn, reads=(), writes=(), dma=None):
        o = Op()
        o.eng, o.fn, o.dma = eng, fn, dma
        o.signal = dma is not None
        o.event = None
        o.idx = len(self.ops)
        deps = set()
        for r in reads:
            w = self.lastw.get(r)
            if w is not None:
                deps.add(w)
        for w_ in writes:
            w = self.lastw.get(w_)
            if w is not None:
                deps.add(w)
            deps.update(self.readers.get(w_, ()))
        deps.discard(o.idx)
        o.deps = deps
        for r in reads:
            self.readers.setdefault(r, []).append(o.idx)
        for w_ in writes:
            self.lastw[w_] = o.idx
            self.readers[w_] = []
        self.ops.append(o)
        return o

    def _skip(self, p, o):
        return p.dma is None and p.eng == o.eng and (p.eng == "pe" or not self.same_engine_sync)

    def end_phase(self):
        nc = self.nc
        ops = self.ops
        last_real = {}
        for o in ops:
            if o.fn is not None and o.dma is None:
                last_real[o.eng] = o.idx
        dma_ops = [o.idx for o in ops if o.dma is not None]
        for e in ENGS:
            o = self.op(e, None)
            o.deps = set(dma_ops) | set(last_real.values())
        for o in ops:
            for d in o.deps:
                p = ops[d]
                if p.dma is None and not self._skip(p, o):
                    p.signal = True
        for o in ops:
            waits = []
            for d in sorted(o.deps):
                p = ops[d]
                if self._skip(p, o):
                    continue
                sem, v = p.event
                k = id(sem)
                if self.seen[o.eng].get(k, 0) >= v:
                    continue
                self.seen[o.eng][k] = v
                waits.append((sem, v))
            o.waits = waits
            if o.dma is not None:
                if o.dma not in self.phase_chan:
                    if self.free_chan[o.eng]:
                        c = self.free_chan[o.eng].pop()
                    else:
                        c = len(self.chan_sem)
                        self.chan_sem[c] = self._mksem("c%d" % c)
                        self.chan_cnt[c] = 0
                    self.phase_chan[o.dma] = c
                    self.chan_eng[c] = o.eng
                c = self.phase_chan[o.dma]
                self.chan_cnt[c] += 16
                o.event = (self.chan_sem[c], self.chan_cnt[c])
            elif o.signal:
                self.eng_cnt[o.eng] += 1
                o.event = (self.eng_sem[o.eng], self.eng_cnt[o.eng])
        per_eng = {e: [o for o in ops if o.eng == e] for e in ENGS}
        self.n_instr += len(ops)

        def emit(e, engobj):
            for o in per_eng[e]:
                for sem, v in o.waits:
                    engobj.wait_ge(sem, v)
                if o.fn is None:
                    continue
                ins = o.fn(engobj)
                if o.event is not None:
                    ins.then_inc(o.event[0], 16 if o.dma is not None else 1)

        with nc.Block() as block:
            @block.tensor
            def _(t):
                emit("pe", t)

            @block.scalar
            def _(t):
                emit("act", t)

            @block.vector
            def _(t):
                emit("dve", t)

            @block.gpsimd
            def _(t):
                emit("pool", t)

            @block.sync
            def _(t):
                emit("sp", t)
        for c_ in self.phase_chan.values():
            self.free_chan[self.chan_eng[c_]].append(c_)
        for g in reversed(self._ctx):
            g.__exit__(None, None, None)
        self._reset_phase()

    def close(self):
        for g in reversed(self._sem_ctx):
            g.__exit__(None, None, None)


class Ctx:
    pass


def ph_ln(P, c, src, dst, g_ap, b_ap, do_ln=True, final_out=None):
    xT = c.xT
    yt = [P.sbuf("ln_y%d" % i, [128, D], F32) for i in range(4)]
    xo = [P.sbuf("ln_o%d" % i, [128, D], F32) for i in range(3)]
    xb = [P.sbuf("ln_b%d" % i, [128, D], BF16) for i in range(2)]
    ident = P.sbuf("ln_id", [128, 128], BF16)
    identf = P.sbuf("ln_idf", [128, 128], F32)
    st = [P.sbuf("ln_st%d" % i, [128, 4, 6], F32) for i in range(3)]
    mv = [P.sbuf("ln_mv%d" % i, [128, 4], F32) for i in range(3)]
    pt = [P.psum("ln_pt%d" % i, [128, 8, 128], BF16) for i in range(4)]
    P.op("sp", lambda e: e.dma_start(out=identf[:], in_=c.consts[0:128, 0:128]), writes=["idf"], dma="idf")
    P.op("dve", lambda e: e.tensor_copy(out=ident[:], in_=identf[:]), reads=["idf"], writes=["id"])
    if do_ln:
        gB = P.sbuf("ln_g", [128, D], F32)
        bB = P.sbuf("ln_bb", [128, D], F32)
        P.op("sp", lambda e: e.dma_start(out=gB[:], in_=g_ap.partition_broadcast(128)), writes=["gB"], dma="gB")
        P.op("sp", lambda e: e.dma_start(out=bB[:], in_=b_ap.partition_broadcast(128)), writes=["bB"], dma="bB")

    def ld(tt):
        y3 = tt % 4
        rows = slice(tt * 128, (tt + 1) * 128)
        P.op("sp", lambda e: e.dma_start(out=yt[y3][:], in_=src[rows, :]), writes=[("yt", y3)], dma=("yt", y3))

    def S1(tt):
        y3, i = tt % 4, tt % 3

        def stats(e):
            for q in range(4):
                ins = e.bn_stats(out=st[i][:, q, :], in_=yt[y3][:, q * 512:(q + 1) * 512])
            return ins
        P.op("dve", stats, reads=[("yt", y3)], writes=[("st", i)])
        P.op("dve", lambda e: e.bn_aggr(out=mv[i][:, 0:2], in_=st[i][:]), reads=[("st", i)], writes=[("mv", i)])
        P.op("dve", lambda e: e.tensor_scalar(out=mv[i][:, 1:2], in0=mv[i][:, 1:2], scalar1=LN_EPS, scalar2=None, op0=ALU.add),
             reads=[("mv", i)], writes=[("mv", i)])
        P.op("act", lambda e: e.activation(out=mv[i][:, 3:4], in_=mv[i][:, 1:2], func=AF.Sqrt), reads=[("mv", i)], writes=[("mv", i)])
        P.op("dve", lambda e: e.reciprocal(out=mv[i][:, 2:3], in_=mv[i][:, 3:4]), reads=[("mv", i)], writes=[("mv", i)])
        P.op("dve", lambda e: e.scalar_tensor_tensor(out=mv[i][:, 3:4], in0=mv[i][:, 0:1], scalar=-1.0, in1=mv[i][:, 2:3],
                                                     op0=ALU.mult, op1=ALU.mult), reads=[("mv", i)], writes=[("mv", i)])

    def norm(tt):
        y3, i = tt % 4, tt % 3
        P.op("act", lambda e: e.activation(out=xo[i][:], in_=yt[y3][:], func=AF.Identity, bias=mv[i][:, 3:4], scale=mv[i][:, 2:3]),
             reads=[("yt", y3), ("mv", i)], writes=[("xo", i)])

    def affine(tt):
        i = tt % 3
        rows = slice(tt * 128, (tt + 1) * 128)
        P.op("dve", lambda e: e.tensor_tensor(out=xo[i][:], in0=xo[i][:], in1=gB[:], op=ALU.mult),
             reads=[("xo", i), "gB"], writes=[("xo", i)])
        P.op("dve", lambda e: e.tensor_tensor(out=xo[i][:], in0=xo[i][:], in1=bB[:], op=ALU.add),
             reads=[("xo", i), "bB"], writes=[("xo", i)])
        dd = dst if final_out is None else final_out
        P.op("sp", lambda e: e.dma_start(out=dd[rows, :], in_=xo[i][:]), reads=[("xo", i)], dma=("xo", i))

    def cast_tr(tt):
        if final_out is not None:
            return
        y3, i, j2 = tt % 4, tt % 3, tt % 2
        srcbuf, srcu = (xo[i], ("xo", i)) if do_ln else (yt[y3], ("yt", y3))
        P.op("act", lambda e: e.copy(out=xb[j2][:], in_=srcbuf[:]), reads=[srcu], writes=[("xb", j2)])
        for hh in range(2):
            pi = (tt * 2 + hh) % 4

            def tr(e, hh=hh, pi=pi):
                for j in range(8):
                    kc = hh * 8 + j
                    ins = e.transpose(out=pt[pi][:, j, :], in_=xb[j2][:, kc * 128:(kc + 1) * 128], identity=ident[:])
                return ins
            P.op("pe", tr, reads=[("xb", j2), "id"], writes=[("pt", pi)])

    def evac(tt):
        if final_out is not None:
            return
        for hh in range(2):
            pi = (tt * 2 + hh) % 4
            if hh == 0:
                f = lambda e, hh=hh, pi=pi: e.copy(out=xT[:, hh * 8:(hh + 1) * 8, tt * 128:(tt + 1) * 128], in_=pt[pi][:])
            else:
                f = lambda e, hh=hh, pi=pi: e.tensor_copy(out=xT[:, hh * 8:(hh + 1) * 8, tt * 128:(tt + 1) * 128], in_=pt[pi][:])
            P.op("act" if hh == 0 else "dve", f, reads=[("pt", pi)], writes=[("xT", tt, hh)])

    ld(0)
    ld(1)
    ld(2)
    if do_ln:
        S1(0)
        for tt in range(TT + 1):
            if tt + 3 < TT:
                ld(tt + 3)
            if tt < TT:
                norm(tt)
            if tt >= 1:
                cast_tr(tt - 1)
            if tt + 1 < TT:
                S1(tt + 1)
            if tt < TT:
                affine(tt)
            if tt >= 1:
                evac(tt - 1)
    else:
        for tt in range(TT):
            if tt + 3 < TT:
                ld(tt + 3)
            cast_tr(tt)
            evac(tt)
    P.end_phase()


def wload(P, dst_ap, src_ap, unit, nsplit=1, kdim=None):
    P.op("pool", lambda e: e.dma_start(out=dst_ap, in_=src_ap), writes=[unit], dma=unit)


def gu_load(P, dst, unit, wgu, fb, FB=512):
    for gu in range(2):
        col0 = gu * DFF + fb * FB
        for half in range(2):
            ks = slice(half * 8, half * 8 + 8)
            src = wgu[half * 1024:(half + 1) * 1024, col0:col0 + FB].rearrange("(kc p) n -> p kc n", p=128)
            P.op("pool", lambda e, gu=gu, ks=ks, src=src: e.dma_start(out=dst[:, gu, ks, :], in_=src),
                 writes=[unit], dma=unit)


def dn_load(P, dst, unit, wd, nk, db, NBd=512):
    kstep = 11 if nk % 11 == 0 else 8
    for k0 in range(0, nk, kstep):
        src = wd[k0 * 128:(k0 + kstep) * 128, db * NBd:(db + 1) * NBd].rearrange("(kc p) n -> p kc n", p=128)
        P.op("pool", lambda e, k0=k0, src=src: e.dma_start(out=dst[:, k0:k0 + kstep, :], in_=src),
             writes=[unit], dma=unit)


def pj_load(P, dst, unit, w, col0, fb, nb):
    for half in range(2):
        ks = slice(half * 8, half * 8 + 8)
        src = w[half * 1024:(half + 1) * 1024, col0 + fb * nb:col0 + (fb + 1) * nb].rearrange("(kc p) n -> p kc n", p=128)
        P.op("pool", lambda e, ks=ks, src=src: e.dma_start(out=dst[:, ks, :], in_=src), writes=[unit], dma=unit)


def wview(c, kind, nk=None, nb=None):
    if kind == "gu":
        return c.wpre[:, 0:2 * KC * 512].rearrange("p (g k n) -> p g k n", g=2, k=KC)
    if kind == "dn":
        return c.wpre[:, 0:nk * 512].rearrange("p (k n) -> p k n", k=nk)
    return c.wpre[:, 0:KC * nb].rearrange("p (k n) -> p k n", k=KC)


def ph_gate_up(P, c, wgu, pre=False, nxt=None):
    xT = c.xT
    NS = 3
    FB = 512
    NB = DFF // FB
    ws = [wview(c, "gu")] + [P.sbuf("gu_w%d" % i, [128, 2, KC, FB], BF16) for i in range(1, NS)]
    wu = ["wpre"] + [("ws", i) for i in range(1, NS)]
    hrow = [P.sbuf("gu_h%d" % i, [128, TT, 4, 128], BF16) for i in range(2)]
    sg = [P.sbuf("gu_s%d" % i, [128, 512], F32) for i in range(1)] * 2
    pg = [P.psum("gu_pg%d" % i, [128, 512], F32) for i in range(2)]
    pu = [P.psum("gu_pu%d" % i, [128, 512], F32) for i in range(2)]
    HTv = c.HT.rearrange("tt p x -> p tt x")
    b0 = max(b_ for b_ in range(NB) if b_ % NS == 0)

    def load(fb):
        if fb == 0 and pre:
            return
        gu_load(P, ws[fb % NS], wu[fb % NS], wgu, fb, FB)

    for fb in range(min(NS, NB)):
        load(fb)
    cnt = 0
    for fb in range(NB):
        sl = fb % NS
        hi = fb % 2
        for fc in range(4):
            for tq in range(4):
                b = cnt % 2
                cnt += 1
                xu = [("xT", tq * 4 + j, hh) for j in range(4) for hh in range(2)]

                def mm(e, sl=sl, fc=fc, tq=tq, b=b):
                    for gu, ps in ((0, pg[b]), (1, pu[b])):
                        for kc in range(KC):
                            ins = e.matmul(ps[:], lhsT=ws[sl][:, gu, kc, fc * 128:(fc + 1) * 128],
                                           rhs=xT[:, kc, tq * 512:(tq + 1) * 512], start=(kc == 0), stop=(kc == KC - 1))
                    return ins
                P.op("pe", mm, reads=[wu[sl]] + xu, writes=[("pg", b), ("pu", b)])
                P.op("act", lambda e, b=b: e.activation(out=sg[b][:], in_=pg[b][:], func=AF.Silu),
                     reads=[("pg", b)], writes=[("sg", 0)])
                P.op("dve", lambda e, b=b, hi=hi, fc=fc, tq=tq: e.tensor_tensor(
                    out=hrow[hi][:, tq * 4:(tq + 1) * 4, fc, :], in0=sg[b][:].rearrange("p (a t) -> p a t", t=128),
                    in1=pu[b][:].rearrange("p (a t) -> p a t", t=128), op=ALU.mult),
                    reads=[("sg", 0), ("pu", b)], writes=[("hrow", hi)])
        P.op("sp", lambda e, hi=hi, fb=fb: e.dma_start(
            out=HTv[:, :, fb * 512:(fb + 1) * 512], in_=hrow[hi][:].rearrange("p tt fc t -> p tt (fc t)")),
            reads=[("hrow", hi)], dma=("hrow", hi))
        if fb + NS < NB:
            load(fb + NS)
        if fb == b0 and nxt is not None:
            nxt()
    P.end_phase()


def ph_down(P, c, wd, nk, act_src, resid, ydst, scale, pre=False, nxt=None):
    NBd = 512
    ws = [wview(c, "dn", nk=nk), P.sbuf("dn_w1", [128, nk, NBd], BF16)]
    wu = ["wpre", ("ws", 1)]
    NHS = 4
    hs = [P.sbuf("dn_h%d" % i, [128, nk * 128], BF16) for i in range(NHS)]
    xr = [P.sbuf("dn_x%d" % i, [128, NBd], F32) for i in range(4)]
    p2 = [P.sbuf("dn_p%d" % i, [128, NBd], F32) for i in range(4)]
    yo = [P.sbuf("dn_y%d" % i, [128, NBd], F32) for i in range(4)]
    py = [P.psum("dn_ps%d" % i, [128, NBd], F32) for i in range(2)]
    NDB = D // NBd
    kstep = 11 if nk % 11 == 0 else 8

    def load(db):
        if db == 0 and pre:
            return
        dn_load(P, ws[db % 2], wu[db % 2], wd, nk, db, NBd)

    PF = 3
    iters = [(db, tt) for db in range(NDB) for tt in range(TT)]

    def loads(n):
        db, tt = iters[n]
        h3, x3 = n % NHS, n % 4
        rows = slice(tt * 128, (tt + 1) * 128)
        cols = slice(db * NBd, (db + 1) * NBd)
        P.op("sp", lambda e, h3=h3, tt=tt: e.dma_start(out=hs[h3][:], in_=act_src[tt]),
             writes=[("hs", h3)], dma=("hs", h3))
        P.op("sp", lambda e, x3=x3, rows=rows, cols=cols: e.dma_start(out=xr[x3][:], in_=resid[rows, cols]),
             writes=[("xr", x3)], dma=("xr", x3))

    load(0)
    for n in range(min(PF, len(iters))):
        loads(n)
    for n, (db, tt) in enumerate(iters):
        if tt == 0 and db + 1 < NDB:
            load(db + 1)
        if tt == 0 and db == NDB - 1 and nxt is not None:
            nxt()
        if n + PF < len(iters):
            loads(n + PF)
        sl = db % 2
        h3, x3, b = n % NHS, n % 4, n % 2
        rows = slice(tt * 128, (tt + 1) * 128)
        cols = slice(db * NBd, (db + 1) * NBd)

        def mm(e, sl=sl, h3=h3, b=b):
            for kc in range(nk):
                ins = e.matmul(py[b][:], lhsT=hs[h3][:, kc * 128:(kc + 1) * 128], rhs=ws[sl][:, kc, :],
                               start=(kc == 0), stop=(kc == nk - 1))
            return ins
        P.op("pe", mm, reads=[wu[sl], ("hs", h3)], writes=[("py", b)])
        P.op("act", lambda e, b=b, x3=x3: e.activation(out=p2[x3][:], in_=py[b][:], func=AF.Copy, scale=float(scale)),
             reads=[("py", b)], writes=[("p2", x3)])
        P.op("dve", lambda e, x3=x3: e.scalar_tensor_tensor(out=yo[x3][:], in0=xr[x3][:], scalar=float(ALPHA),
                                                            in1=p2[x3][:], op0=ALU.mult, op1=ALU.add),
             reads=[("xr", x3), ("p2", x3)], writes=[("yo", x3)])
        P.op("sp", lambda e, x3=x3, rows=rows, cols=cols: e.dma_start(out=ydst[rows, cols], in_=yo[x3][:]),
             reads=[("yo", x3)], dma=("yo", x3))
    P.end_phase()


def ph_fm(P, c, w, col0, ncols, dst, dst_dtype, scale_of_chunk=None, pre=False, nxt=None):
    xT = c.xT
    NS = 3
    FB = 512
    NB = ncols // FB
    ws = [wview(c, "pj", nb=FB)] + [P.sbuf("fm_w%d" % i, [128, KC, FB], BF16) for i in range(1, NS)]
    wu = ["wpre"] + [("ws", i) for i in range(1, NS)]
    row = [P.sbuf("fm_r%d" % i, [128, 4, S], dst_dtype) for i in range(2)]
    ps = [P.psum("fm_p%d" % i, [128, 512], F32) for i in range(4)]
    dstv = dst.rearrange("c p t -> p c t")
    b0 = max(b_ for b_ in range(NB) if b_ % NS == 0)

    def load(fb):
        if fb == 0 and pre:
            return
        pj_load(P, ws[fb % NS], wu[fb % NS], w, col0, fb, FB)

    for fb in range(min(NS, NB)):
        load(fb)
    cnt = 0
    for fb in range(NB):
        sl = fb % NS
        ri = fb % 2
        for fc in range(4):
            sc = 1.0 if scale_of_chunk is None else scale_of_chunk(fb * 4 + fc)
            for tq in range(4):
                b = cnt % 4
                cnt += 1
                xu = [("xT", tq * 4 + j, hh) for j in range(4) for hh in range(2)]

                def mm(e, sl=sl, fc=fc, tq=tq, b=b):
                    for kc in range(KC):
                        ins = e.matmul(ps[b][:], lhsT=ws[sl][:, kc, fc * 128:(fc + 1) * 128],
                                       rhs=xT[:, kc, tq * 512:(tq + 1) * 512], start=(kc == 0), stop=(kc == KC - 1))
                    return ins
                P.op("pe", mm, reads=[wu[sl]] + xu, writes=[("ps", b)])
                if cnt % 2 == 0:
                    P.op("act", lambda e, b=b, ri=ri, fc=fc, tq=tq, sc=sc: e.activation(
                        out=row[ri][:, fc, tq * 512:(tq + 1) * 512], in_=ps[b][:], func=AF.Copy, scale=float(sc)),
                        reads=[("ps", b)], writes=[("row", ri)])
                else:
                    P.op("dve", lambda e, b=b, ri=ri, fc=fc, tq=tq, sc=sc: e.tensor_scalar(
                        out=row[ri][:, fc, tq * 512:(tq + 1) * 512], in0=ps[b][:], scalar1=float(sc), scalar2=None,
                        op0=ALU.mult), reads=[("ps", b)], writes=[("row", ri)])
        P.op("sp", lambda e, ri=ri, fb=fb: e.dma_start(out=dstv[:, fb * 4:(fb + 1) * 4, :], in_=row[ri][:]),
             reads=[("row", ri)], dma=("row", ri))
        if fb + NS < NB:
            load(fb + NS)
        if fb == b0 and nxt is not None:
            nxt()
    P.end_phase()


def ph_tm(P, c, w, col0, ncols, store, dst_dtype, nb=512, NS=3, pre=False, nxt=None):
    xT = c.xT
    NB = ncols // nb
    ws = [wview(c, "pj", nb=nb)] + [P.sbuf("tm_w%d" % i, [128, KC, nb], BF16) for i in range(1, NS)]
    wu = ["wpre"] + [("ws", i) for i in range(1, NS)]
    ot = [P.sbuf("tm_o%d" % i, [128, nb], dst_dtype) for i in range(3)]
    ps = [P.psum("tm_p%d" % i, [128, nb], F32) for i in range(4)]
    b0 = max(b_ for b_ in range(NB) if b_ % NS == 0)

    def load(fb):
        if fb == 0 and pre:
            return
        pj_load(P, ws[fb % NS], wu[fb % NS], w, col0, fb, nb)

    for fb in range(min(NS, NB)):
        load(fb)
    cnt = 0
    for fb in range(NB):
        sl = fb % NS
        for tt in range(TT):
            b = cnt % 4
            o3 = cnt % 3
            cnt += 1

            def mm(e, sl=sl, tt=tt, b=b):
                for kc in range(KC):
                    ins = e.matmul(ps[b][:], lhsT=xT[:, kc, tt * 128:(tt + 1) * 128], rhs=ws[sl][:, kc, :],
                                   start=(kc == 0), stop=(kc == KC - 1))
                return ins
            P.op("pe", mm, reads=[wu[sl], ("xT", tt, 0), ("xT", tt, 1)], writes=[("ps", b)])
            if cnt % 2 == 0:
                P.op("act", lambda e, b=b, o3=o3: e.copy(out=ot[o3][:], in_=ps[b][:]), reads=[("ps", b)], writes=[("ot", o3)])
            else:
                P.op("dve", lambda e, b=b, o3=o3: e.tensor_copy(out=ot[o3][:], in_=ps[b][:]), reads=[("ps", b)], writes=[("ot", o3)])
            P.op("sp", lambda e, o3=o3, fb=fb, tt=tt: store(e, ot[o3], fb, tt), reads=[("ot", o3)], dma=("ot", o3))
        if fb + NS < NB:
            load(fb + NS)
        if fb == b0 and nxt is not None:
            nxt()
    P.end_phase()


def ph_attn(P, c):
    cf = P.sbuf("at_cf", [128, 2304], F32)
    cb = P.sbuf("at_cb", [128, 2304], BF16)
    P.op("sp", lambda e: e.dma_start(out=cf[:], in_=c.consts[:, 128:2432]), writes=["cf"], dma="cf")
    P.op("dve", lambda e: e.tensor_copy(out=cb[:], in_=cf[:]), reads=["cf"], writes=["cb"])
    mask = lambda i: cb[:, i * 512:(i + 1) * 512]
    negtri = cb[:, 2048:2176]
    negone = cb[:, 2176:2304]
    qT = [P.sbuf("at_q%d" % i, [128, S], BF16) for i in range(2)]
    kT = [P.sbuf("at_k%d" % i, [128, S], BF16) for i in range(2)]
    vv = [P.sbuf("at_v%d" % i, [128, 16, 128], BF16) for i in range(2)]
    orow = [P.sbuf("at_o%d" % i, [128, TT, 128], BF16) for i in range(2)]
    ef = [P.sbuf("at_e%d" % i, [128, 512], F32) for i in range(3)]
    pb = [P.sbuf("at_pb%d" % i, [128, 512], BF16) for i in range(3)]
    rs = P.sbuf("at_rs", [128, 512], F32)
    rsb = [P.sbuf("at_rsb%d" % i, [128, 512], BF16) for i in range(3)]
    att = [P.sbuf("at_a%d" % i, [128, 512], BF16) for i in range(3)]
    pA = [P.psum("at_pA%d" % i, [128, 512], F32) for i in range(2)]
    pB = [P.psum("at_pB%d" % i, [128, 512], F32) for i in range(2)]
    pO = [P.psum("at_pO%d" % i, [128, 512], F32) for i in range(2)]
    OTv = c.OT4.rearrange("tt p (h t) -> p tt h t", h=NH)

    def loadh(h):
        i = h % 2
        P.op("sp", lambda e: e.dma_start(out=qT[i][:], in_=c.QT[h]), writes=[("q", i)], dma=("q", i))
        P.op("sp", lambda e: e.dma_start(out=kT[i][:], in_=c.KT[h]), writes=[("k", i)], dma=("k", i))
        P.op("sp", lambda e: e.dma_start(out=vv[i][:], in_=c.V4[h]), writes=[("v", i)], dma=("v", i))

    loadh(0)
    n = 0
    for h in range(NH):
        hi = h % 2
        if h + 1 < NH:
            loadh(h + 1)
        blocks = [(cq, kb) for cq in range(4) for kb in range(4 * cq + 3, -1, -1)]
        nb = len(blocks)
        base = n

        def issue_A(j, hi=hi, base=base, blocks=blocks):
            cq, kb = blocks[j]
            a = (base + j) % 2
            P.op("pe", lambda e, a=a, cq=cq, kb=kb: e.matmul(
                pA[a][:], lhsT=kT[hi][:, kb * 128:(kb + 1) * 128], rhs=qT[hi][:, cq * 512:(cq + 1) * 512],
                start=True, stop=True), reads=[("q", hi), ("k", hi)], writes=[("pA", a)])

        def act_e(j, hi=hi, base=base, blocks=blocks):
            g = base + j
            a, p3 = g % 2, g % 3
            P.op("act", lambda e, a=a, p3=p3: e.activation(out=ef[p3][:], in_=pA[a][:], func=AF.Exp),
                 reads=[("pA", a)], writes=[("ef", p3)])

        def act_P(j, hi=hi, base=base, blocks=blocks):
            cq, kb = blocks[j]
            g = base + j
            a, p3 = g % 2, g % 3
            first = kb == 4 * cq + 3
            diag = kb >= 4 * cq
            P.op("act", lambda e, a=a, p3=p3: e.activation(out=pb[p3][:], in_=ef[p3][:], func=AF.Ln, bias=1.0),
                 reads=[("ef", p3)], writes=[("pb", p3)])
            if diag:
                P.op("dve", lambda e, p3=p3, i=kb - 4 * cq: e.tensor_tensor(out=pb[p3][:], in0=pb[p3][:], in1=mask(i), op=ALU.mult),
                     reads=[("pb", p3), "cb"], writes=[("pb", p3)])
            if kb > 0:
                if first:
                    P.op("dve", lambda e, p3=p3: e.tensor_copy(out=rs[:], in_=pb[p3][:]), reads=[("pb", p3)], writes=["rs"])
                else:
                    P.op("dve", lambda e, p3=p3: e.tensor_tensor(out=rs[:], in0=rs[:], in1=pb[p3][:], op=ALU.add),
                         reads=[("pb", p3), "rs"], writes=["rs"])
                P.op("dve", lambda e, r=g % 3: e.tensor_copy(out=rsb[r][:], in_=rs[:]), reads=["rs"], writes=[("rsb", g % 3)])

        def issue_B(j, hi=hi, base=base, blocks=blocks):
            cq, kb = blocks[j]
            g = base + j
            b, p3 = g % 2, g % 3
            first = kb == 4 * cq + 3

            def f(e):
                e.matmul(pB[b][:], lhsT=kT[hi][:, kb * 128:(kb + 1) * 128], rhs=qT[hi][:, cq * 512:(cq + 1) * 512],
                         start=True, stop=False)
                ins = e.matmul(pB[b][:], lhsT=negtri, rhs=pb[p3][:], start=False, stop=first)
                if not first:
                    ins = e.matmul(pB[b][:], lhsT=negone, rhs=rsb[(g - 1) % 3][:], start=False, stop=True)
                return ins
            rd = [("q", hi), ("k", hi), ("pb", p3), "cb"] + ([] if first else [("rsb", (g - 1) % 3)])
            P.op("pe", f, reads=rd, writes=[("pB", b)])

        def act_att(j, hi=hi, base=base, blocks=blocks):
            cq, kb = blocks[j]
            g = base + j
            b, a3 = g % 2, g % 3
            P.op("act", lambda e: e.activation(out=att[a3][:], in_=pB[b][:], func=AF.Exp),
                 reads=[("pB", b)], writes=[("att", a3)])
            if kb >= 4 * cq:
                P.op("dve", lambda e, i=kb - 4 * cq: e.tensor_tensor(out=att[a3][:], in0=att[a3][:], in1=mask(i), op=ALU.mult),
                     reads=[("att", a3), "cb"], writes=[("att", a3)])

        def issue_AV(j, hi=hi, base=base, blocks=blocks):
            cq, kb = blocks[j]
            g = base + j
            a3 = g % 3
            o = cq % 2
            first = kb == 4 * cq + 3
            P.op("pe", lambda e: e.matmul(pO[o][:], lhsT=vv[hi][:, kb, :], rhs=att[a3][:], start=first, stop=(kb == 0)),
                 reads=[("v", hi), ("att", a3)], writes=[("pO", o)])
            if kb == 0:
                P.op("dve", lambda e: e.tensor_copy(out=orow[hi][:, cq * 4:(cq + 1) * 4, :],
                                                    in_=pO[o][:].rearrange("p (a t) -> p a t", t=128)),
                     reads=[("pO", o)], writes=[("orow", hi)])

        for j0 in range(min(3, nb)):
            issue_A(j0) if j0 < 2 else None
        act_e(0)
        if nb > 1:
            act_e(1)
        if nb > 2:
            issue_A(2)
        act_P(0)
        for j in range(nb):
            if j + 3 < nb:
                issue_A(j + 3)
            if j + 2 < nb:
                act_e(j + 2)
            if j + 1 < nb:
                act_P(j + 1)
            issue_B(j)
            if j >= 1:
                issue_AV(j - 1)
            act_att(j)
        issue_AV(nb - 1)
        n += nb
        P.op("sp", lambda e, hi=hi, h=h: e.dma_start(out=OTv[:, :, h, :], in_=orow[hi][:]),
             reads=[("orow", hi)], dma=("orow", hi))
    P.end_phase()


def ph_ssd_conv(P, c):
    cw = P.sbuf("cv_w", [128, 48, 5], F32)
    idf = P.sbuf("cv_idf", [128, 128], F32)
    idb = P.sbuf("cv_idb", [128, 128], BF16)
    buf = [P.sbuf("cv_x%d" % i, [128, 3 + S], F32) for i in range(3)]
    acc = [P.sbuf("cv_a%d" % i, [128, S], F32) for i in range(3)]
    so = [P.sbuf("cv_s%d" % i, [128, S], F32) for i in range(3)]
    sb = [P.sbuf("cv_b%d" % i, [128, S], BF16) for i in range(3)]
    stg = [P.sbuf("cv_g%d" % i, [128, TT, 128], F32) for i in range(3)]
    stgb = [P.sbuf("cv_h%d" % i, [128, TT, 128], BF16) for i in range(3)]
    pt = [P.psum("cv_p%d" % i, [128, 4, 128], F32) for i in range(4)]
    ptb = [P.psum("cv_q%d" % i, [128, 8, 128], BF16) for i in range(2)]
    P.op("sp", lambda e: e.dma_start(out=cw[:].rearrange("p a b -> p (a b)"), in_=c.cwl), writes=["cw"], dma="cw")
    P.op("sp", lambda e: e.dma_start(out=idf[:], in_=c.consts[:, 0:128]), writes=["idf"], dma="idf")
    P.op("dve", lambda e: e.tensor_copy(out=idb[:], in_=idf[:]), reads=["idf"], writes=["idb"])
    for i in range(3):
        P.op("dve", lambda e, i=i: e.memset(buf[i][:, 0:3], 0.0), writes=[("buf", i)])
    XSv = c.XS.rearrange("(tt p) c -> p tt c", p=128)
    BTKv = c.BTOK.rearrange("(tt p) c -> p tt c", p=128)
    def ldc(ch):
        i = ch % 3
        P.op("sp", lambda e, i=i, ch=ch: e.dma_start(out=buf[i][:, 3:3 + S], in_=c.XBCT[ch]), writes=[("buf", i)], dma=("buf", i))

    def SA(ch):
        i = ch % 3
        P.op("act", lambda e: e.activation(out=acc[i][:], in_=buf[i][:, 0:S], func=AF.Identity,
                                           bias=cw[:, ch, 4:5], scale=cw[:, ch, 0:1]),
             reads=[("buf", i), "cw"], writes=[("acc", i)])
        for k in range(1, 4):
            P.op("dve", lambda e, k=k: e.scalar_tensor_tensor(
                out=acc[i][:], in0=buf[i][:, k:k + S], scalar=cw[:, ch, k:k + 1], in1=acc[i][:], op0=ALU.mult, op1=ALU.add),
                reads=[("buf", i), ("acc", i), "cw"], writes=[("acc", i)])

    def SB(ch):
        i = ch % 3
        if ch < 32:
            P.op("act", lambda e: e.activation(out=so[i][:], in_=acc[i][:], func=AF.Silu), reads=[("acc", i)], writes=[("so", i)])
            for q in range(4):
                def tr(e, q=q):
                    for j in range(4):
                        tt = q * 4 + j
                        ins = e.transpose(out=pt[q][:, j, :], in_=so[i][:, tt * 128:(tt + 1) * 128], identity=idf[:])
                    return ins
                P.op("pe", tr, reads=[("so", i), "idf"], writes=[("pt", q)])
                if q % 2 == 0:
                    P.op("act", lambda e, q=q: e.copy(out=stg[i][:, q * 4:(q + 1) * 4, :], in_=pt[q][:]), reads=[("pt", q)], writes=[("stg", i)])
                else:
                    P.op("dve", lambda e, q=q: e.tensor_copy(out=stg[i][:, q * 4:(q + 1) * 4, :], in_=pt[q][:]), reads=[("pt", q)], writes=[("stg", i)])
            P.op("sp", lambda e: e.dma_start(out=XSv[:, :, ch * 128:(ch + 1) * 128], in_=stg[i][:]), reads=[("stg", i)], dma=("stg", i))
        else:
            g = (ch - 32) % 8
            P.op("act", lambda e: e.activation(out=sb[i][:], in_=acc[i][:], func=AF.Silu), reads=[("acc", i)], writes=[("sb", i)])
            dstT = c.BT if ch < 40 else c.CT
            P.op("sp", lambda e: e.dma_start(out=dstT[g], in_=sb[i][:]), reads=[("sb", i)], dma=("sb", i))
            if ch < 40:
                for q in range(2):
                    def trb(e, q=q):
                        for j in range(8):
                            tt = q * 8 + j
                            ins = e.transpose(out=ptb[q][:, j, :], in_=sb[i][:, tt * 128:(tt + 1) * 128], identity=idb[:])
                        return ins
                    P.op("pe", trb, reads=[("sb", i), "idb"], writes=[("ptb", q)])
                    P.op("act", lambda e, q=q: e.copy(out=stgb[i][:, q * 8:(q + 1) * 8, :], in_=ptb[q][:]), reads=[("ptb", q)], writes=[("stgb", i)])
                P.op("sp", lambda e: e.dma_start(out=BTKv[:, :, g * 128:(g + 1) * 128], in_=stgb[i][:]), reads=[("stgb", i)], dma=("stgb", i))

    ldc(0)
    ldc(1)
    SA(0)
    for ch in range(48):
        if ch + 2 < 48:
            ldc(ch + 2)
        if ch + 1 < 48:
            SA(ch + 1)
        SB(ch)
    P.end_phase()


def ph_ssd_dt(P, c, dt_bias, a_log):
    dbias = P.sbuf("dt_b", [128, 64], F32)
    aB = P.sbuf("dt_a", [128, 64], F32)
    U2 = P.sbuf("dt_u", [128, 256], F32)
    P.op("sp", lambda e: e.dma_start(out=dbias[:], in_=dt_bias.partition_broadcast(128)), writes=["dbias"], dma="dbias")
    P.op("sp", lambda e: e.dma_start(out=aB[:], in_=a_log.partition_broadcast(128)), writes=["aB"], dma="aB")
    P.op("sp", lambda e: e.dma_start(out=U2[:], in_=c.consts[:, 2432:2688]), writes=["U2"], dma="U2")
    P.op("act", lambda e: e.activation(out=aB[:], in_=aB[:], func=AF.Exp), reads=["aB"], writes=["aB"])
    P.op("dve", lambda e: e.tensor_scalar(out=aB[:], in0=aB[:], scalar1=-1.0, scalar2=None, op0=ALU.mult), reads=["aB"], writes=["aB"])
    t = {k: [P.sbuf("dt_%s%d" % (k, i), [128, 64], F32) for i in range(2)] for k in ("r", "x", "e", "dt", "dta", "ac", "dd", "dte", "wdt", "E")}
    pac = [P.psum("dt_pa%d" % i, [128, 64], F32) for i in range(2)]
    pla = [P.psum("dt_pl%d" % i, [128, 64], F32) for i in range(2)]
    for tt in range(TT):
        i = tt % 2
        rows = slice(tt * 128, (tt + 1) * 128)
        u = lambda k, i=i: (k, i)
        P.op("sp", lambda e, i=i, rows=rows: e.dma_start(out=t["r"][i][:], in_=c.DTR[rows, :]), writes=[u("r")], dma=u("r"))
        P.op("dve", lambda e, i=i: e.tensor_tensor(out=t["x"][i][:], in0=t["r"][i][:], in1=dbias[:], op=ALU.add), reads=[u("r"), "dbias"], writes=[u("x")])
        P.op("act", lambda e, i=i: e.activation(out=t["e"][i][:], in_=t["x"][i][:], func=AF.Exp), reads=[u("x")], writes=[u("e")])
        P.op("act", lambda e, i=i: e.activation(out=t["dt"][i][:], in_=t["e"][i][:], func=AF.Ln, bias=1.0), reads=[u("e")], writes=[u("dt")])
        P.op("dve", lambda e, i=i: e.tensor_tensor(out=t["dta"][i][:], in0=t["dt"][i][:], in1=aB[:], op=ALU.mult), reads=[u("dt"), "aB"], writes=[u("dta")])
        P.op("pe", lambda e, i=i: e.matmul(pac[i][:], lhsT=U2[:, 0:128], rhs=t["dta"][i][:], start=True, stop=True), reads=[u("dta"), "U2"], writes=[u("pac")])
        P.op("pe", lambda e, i=i: e.matmul(pla[i][:], lhsT=U2[:, 128:256], rhs=t["dta"][i][:], start=True, stop=True), reads=[u("dta"), "U2"], writes=[u("pla")])
        P.op("act", lambda e, i=i: e.copy(out=t["ac"][i][:], in_=pac[i][:]), reads=[u("pac")], writes=[u("ac")])
        P.op("dve", lambda e, i=i: e.tensor_tensor(out=t["dd"][i][:], in0=pla[i][:], in1=t["ac"][i][:], op=ALU.subtract), reads=[u("pla"), u("ac")], writes=[u("dd")])
        P.op("act", lambda e, i=i: e.activation(out=t["dte"][i][:], in_=t["dd"][i][:], func=AF.Exp), reads=[u("dd")], writes=[u("dte")])
        P.op("dve", lambda e, i=i: e.tensor_tensor(out=t["wdt"][i][:], in0=t["dt"][i][:], in1=t["dte"][i][:], op=ALU.mult), reads=[u("dt"), u("dte")], writes=[u("wdt")])
        P.op("act", lambda e, i=i: e.activation(out=t["E"][i][:], in_=t["ac"][i][:], func=AF.Exp), reads=[u("ac")], writes=[u("E")])
        for k, dstd in (("dt", 0), ("wdt", 1), ("E", 2), ("ac", 3)):
            P.op("sp", lambda e, i=i, k=k, dstd=dstd, rows=rows: e.dma_start(out=c.SM[dstd, rows, :], in_=t[k][i][:]), reads=[u(k)], dma=u(k))
    P.end_phase()


def ph_ssd_main(P, c, ssd_d, ssd_norm_g):
    H = SSD_H
    mk = P.sbuf("sm_mk", [128, 64], F32)
    idf = P.sbuf("sm_idf", [128, 128], F32)
    idb = P.sbuf("sm_idb", [128, 128], BF16)
    dsk = P.sbuf("sm_dsk", [128, H], F32)
    ngB = P.sbuf("sm_ng", [128, SSD_DI], F32)
    hst = P.sbuf("sm_hst", [128, SSD_DI], F32)
    hb = P.sbuf("sm_hb", [128, SSD_DI], BF16)
    M2 = P.sbuf("sm_M2", [128, 128, H], BF16)
    ctp = P.sbuf("sm_ctp", [128, 8, 2, 128], BF16)
    bpd = P.sbuf("sm_bpd", [128, 2, 1024], BF16)
    xs = P.sbuf("sm_xs", [128, H, 64], F32)
    ztg = [P.sbuf("sm_zt%d" % i, [128, 512], F32) for i in range(2)]
    bt = P.sbuf("sm_bt", [128, 8, 128], BF16)
    ct = P.sbuf("sm_ct", [128, 8, 128], BF16)
    btok = P.sbuf("sm_btok", [128, 1024], BF16)
    sm = [P.sbuf("sm_sm%d" % i, [128, 4, H], F32) for i in range(2)]
    arow = P.sbuf("sm_arow", [128, 64, H], F32)
    cdr = [P.sbuf("sm_cdr%d" % i, [128, 2, H], F32) for i in range(2)]
    xdt = P.sbuf("sm_xdt", [128, H, 64], BF16)
    xw = P.sbuf("sm_xw", [128, H, 64], BF16)
    cbm = P.sbuf("sm_cbm", [128, 64, 8], F32)
    tA = [P.sbuf("sm_tA%d" % i, [128, 8, 64], F32) for i in range(8)]
    tB = P.sbuf("sm_tB", [128, 8, 64], F32)
    junk = P.sbuf("sm_junk", [128, 512], BF16)
    ssq = P.sbuf("sm_ssq", [128, 8], F32)
    rstd = P.sbuf("sm_rstd", [128, 8], F32)
    yn = P.sbuf("sm_yn", [128, SSD_DI], BF16)
    ytT = P.sbuf("sm_ytT", [128, 32, 128], BF16)
    pcb = P.psum("sm_pcb", [128, 128], F32)
    pyd = [P.psum("sm_pyd%d" % i, [128, 8, 64], F32) for i in range(2)]
    pyo = [P.psum("sm_pyo%d" % i, [128, 8, 64], F32) for i in range(2)]
    pst = P.psum("sm_pst", [128, 8, 64], F32)
    ptr = [P.psum("sm_ptr%d" % i, [128, 8, 128], BF16) for i in range(2)]
    P.op("sp", lambda e: e.dma_start(out=mk[:], in_=c.consts[:, 2688:2752]), writes=["mk"], dma="mk")
    P.op("sp", lambda e: e.dma_start(out=idf[:], in_=c.consts[:, 0:128]), writes=["idf"], dma="idf")
    P.op("dve", lambda e: e.tensor_copy(out=idb[:], in_=idf[:]), reads=["idf"], writes=["idb"])
    P.op("sp", lambda e: e.dma_start(out=dsk[:], in_=ssd_d.partition_broadcast(128)), writes=["dsk"], dma="dsk")
    P.op("sp", lambda e: e.dma_start(out=ngB[:], in_=ssd_norm_g.partition_broadcast(128)), writes=["ngB"], dma="ngB")
    P.op("dve", lambda e: e.memset(hst[:], 0.0), writes=["hst"])
    P.op("dve", lambda e: e.memset(hb[:], 0.0), writes=["hb"])
    P.op("dve", lambda e: e.memset(M2[:], 0.0), writes=["M2"])
    P.op("dve", lambda e: e.memset(ctp[:], 0.0), writes=["ctp"])
    P.op("dve", lambda e: e.memset(bpd[:], 0.0), writes=["bpd"])
    ACc = c.SM[3].rearrange("(cc i) h -> cc i h", i=64)
    bc = lambda ap: ap.unsqueeze(2).to_broadcast([128, H, 64])

    def zload(tt, g):
        zb = g % 2
        rows = slice(tt * 128, (tt + 1) * 128)
        P.op("sp", lambda e: e.dma_start(out=ztg[zb][:], in_=c.ZS[rows, g * 512:(g + 1) * 512]), writes=[("zt", zb)], dma=("zt", zb))

    def load_arow(tt):
        for cc in range(2):
            P.op("sp", lambda e, cc=cc: e.dma_start(
                out=arow[cc * 64:(cc + 1) * 64].rearrange("p i h -> p (i h)"),
                in_=ACc[2 * tt + cc:2 * tt + cc + 1].rearrange("a i h -> a (i h)").partition_broadcast(64)), writes=["arow"], dma="arow")

    def loads(tt):
        rows = slice(tt * 128, (tt + 1) * 128)
        P.op("sp", lambda e: e.dma_start(out=sm[tt % 2][:], in_=c.SM[:, rows, :].rearrange("k p h -> p k h")), writes=[("sm", tt % 2)], dma=("sm", tt % 2))
        P.op("sp", lambda e: e.dma_start(out=xs[:].rearrange("p h d -> p (h d)"), in_=c.XS[rows, :]), writes=["xs"], dma="xs")
        P.op("sp", lambda e: e.dma_start(out=bt[:], in_=c.BT[:, :, rows].rearrange("g n t -> n g t")), writes=["bt"], dma="bt")
        P.op("sp", lambda e: e.dma_start(out=ct[:], in_=c.CT[:, :, rows].rearrange("g n t -> n g t")), writes=["ct"], dma="ct")
        P.op("sp", lambda e: e.dma_start(out=btok[:], in_=c.BTOK[rows, :]), writes=["btok"], dma="btok")
        for cc in range(2):
            P.op("sp", lambda e, cc=cc: e.dma_start(out=cdr[tt % 2][:, cc, :], in_=ACc[2 * tt + cc, 63:64, :].partition_broadcast(128)),
                 writes=[("cdr", tt % 2)], dma=("cdr", tt % 2))
        zload(tt, 0)

    def prep(tt):
        P.op("dve", lambda e: e.tensor_tensor(out=arow[:], in0=arow[:], in1=sm[tt % 2][:, 3, :].unsqueeze(1).to_broadcast([128, 64, H]), op=ALU.subtract),
             reads=["arow", ("sm", tt % 2)], writes=["arow"])
        P.op("dve", lambda e: e.tensor_scalar(out=arow[:], in0=arow[:], scalar1=0.0, scalar2=None, op0=ALU.min), reads=["arow"], writes=["arow"])
        P.op("act", lambda e: e.activation(out=arow[:], in_=arow[:], func=AF.Exp), reads=["arow"], writes=["arow"])
        P.op("act", lambda e: e.activation(out=cdr[tt % 2][:], in_=cdr[tt % 2][:], func=AF.Exp), reads=[("cdr", tt % 2)], writes=[("cdr", tt % 2)])

    def prep_b(tt):
        P.op("dve", lambda e: e.tensor_tensor(out=xdt[:], in0=xs[:], in1=bc(sm[tt % 2][:, 0, :]), op=ALU.mult), reads=["xs", ("sm", tt % 2)], writes=["xdt"])
        P.op("dve", lambda e: e.tensor_tensor(out=xw[:], in0=xs[:], in1=bc(sm[tt % 2][:, 1, :]), op=ALU.mult), reads=["xs", ("sm", tt % 2)], writes=["xw"])
        for cc in range(2):
            ps_ = slice(cc * 64, (cc + 1) * 64)
            P.op("act", lambda e, cc=cc, ps_=ps_: e.copy(out=ctp[:, :, cc, ps_], in_=ct[:, :, ps_]), reads=["ct"], writes=["ctp"])
            P.op("act", lambda e, cc=cc, ps_=ps_: e.copy(out=bpd[ps_, cc, :], in_=btok[ps_, :]), reads=["btok"], writes=["bpd"])

    def A(tt, g):
        gs = slice(g * 8, (g + 1) * 8)
        b = g % 2
        P.op("pe", lambda e: e.matmul(pcb[:], lhsT=bt[:, g, :], rhs=ct[:, g, :], start=True, stop=True), reads=["bt", "ct"], writes=["pcb"])
        for cc in range(2):
            ps_ = slice(cc * 64, (cc + 1) * 64)
            P.op("dve", lambda e, ps_=ps_: e.tensor_tensor(out=cbm[ps_, :, g], in0=pcb[ps_, ps_], in1=mk[ps_, :], op=ALU.mult),
                 reads=["pcb", "mk"], writes=["cbm"])
        for cc in range(2):
            ps_ = slice(cc * 64, (cc + 1) * 64)
            P.op("dve", lambda e, ps_=ps_: e.tensor_tensor(
                out=M2[ps_, ps_, gs], in0=arow[ps_, :, gs], in1=cbm[ps_, :, g:g + 1].to_broadcast([64, 64, 8]), op=ALU.mult),
                reads=["arow", "cbm"], writes=[("M2", g)])

        def ydiag(e):
            for r in range(8):
                h = g * 8 + r
                ins = e.matmul(pyd[b][:, r, :], lhsT=M2[:, :, h], rhs=xdt[:, h, :], start=True, stop=True)
            return ins
        P.op("pe", ydiag, reads=[("M2", g), "xdt"], writes=[("pyd", b)])
        hv = hst[:, g * 512:(g + 1) * 512].rearrange("p (r d) -> p r d", d=64)
        for cc in range(2):
            P.op("pe", lambda e, cc=cc: e.matmul(
                pyo[b][:].rearrange("p r d -> p (r d)"), lhsT=ctp[:, g, cc, :], rhs=hb[:, g * 512:(g + 1) * 512],
                start=(cc == 0), stop=(cc == 1)), reads=["ctp", ("hb", g)], writes=[("pyo", b)])
            P.op("pe", lambda e, cc=cc: e.matmul(
                pst[:].rearrange("p r d -> p (r d)"), lhsT=bpd[:, cc, g * 128:(g + 1) * 128],
                rhs=xw[:, g * 8:(g + 1) * 8, :].rearrange("p r d -> p (r d)"),
                start=True, stop=True), reads=["bpd", "xw"], writes=["pst"])
            P.op("dve", lambda e, cc=cc: e.tensor_tensor(
                out=hv, in0=hv, in1=cdr[tt % 2][:, cc, g * 8:(g + 1) * 8].unsqueeze(2).to_broadcast([128, 8, 64]), op=ALU.mult),
                reads=[("hst", g), ("cdr", tt % 2)], writes=[("hst", g)])
            P.op("dve", lambda e: e.tensor_tensor(out=hv, in0=hv, in1=pst[:], op=ALU.add),
                 reads=[("hst", g), "pst"], writes=[("hst", g)])
            P.op("act", lambda e: e.copy(out=hb[:, g * 512:(g + 1) * 512], in_=hst[:, g * 512:(g + 1) * 512]),
                 reads=[("hst", g)], writes=[("hb", g)])

    def B1(tt, g):
        gs = slice(g * 8, (g + 1) * 8)
        b, zb = g % 2, g % 2
        if g + 1 < 8:
            zload(tt, g + 1)
        Eb = sm[tt % 2][:, 2, g * 8:(g + 1) * 8].unsqueeze(2).to_broadcast([128, 8, 64])
        Db = dsk[:, g * 8:(g + 1) * 8].unsqueeze(2).to_broadcast([128, 8, 64])
        tg = tA[g]
        tgf = tg[:].rearrange("p r d -> p (r d)")
        u = ("tA", g)
        P.op("act", lambda e: e.activation(out=ztg[zb][:], in_=ztg[zb][:], func=AF.Silu), reads=[("zt", zb)], writes=[("zt", zb)])
        P.op("dve", lambda e: e.tensor_tensor(out=tg[:], in0=pyo[b][:], in1=Eb, op=ALU.mult), reads=[("pyo", b), ("sm", tt % 2)], writes=[u])
        P.op("dve", lambda e: e.tensor_tensor(out=tg[:], in0=tg[:], in1=pyd[b][:], op=ALU.add), reads=[u, ("pyd", b)], writes=[u])
        P.op("dve", lambda e: e.tensor_tensor(out=tB[:], in0=xs[:, gs, :], in1=Db, op=ALU.mult), reads=["xs", "dsk"], writes=["tB"])
        P.op("dve", lambda e: e.tensor_tensor(out=tg[:], in0=tg[:], in1=tB[:], op=ALU.add), reads=[u, "tB"], writes=[u])
        P.op("dve", lambda e: e.tensor_tensor(out=tgf, in0=tgf, in1=ztg[zb][:], op=ALU.mult), reads=[u, ("zt", zb)], writes=[u])
        P.op("act", lambda e: e.activation(out=junk[:], in_=tgf, func=AF.Square, accum_out=ssq[:, g:g + 1]),
             reads=[u], writes=["junk", ("ssq", g)])

    def fin(tt):
        su = [("ssq", g) for g in range(8)]
        P.op("act", lambda e: e.activation(out=rstd[:], in_=ssq[:], func=AF.Ln, scale=1.0 / 512.0, bias=RMS_EPS), reads=su, writes=["rstd"])
        P.op("act", lambda e: e.activation(out=rstd[:], in_=rstd[:], func=AF.Exp, scale=-0.5), reads=["rstd"], writes=["rstd"])

    def fin_yn(tt):
        for g in range(8):
            P.op("dve", lambda e, g=g: e.scalar_tensor_tensor(
                out=yn[:, g * 512:(g + 1) * 512], in0=tA[g][:].rearrange("p r d -> p (r d)"), scalar=rstd[:, g:g + 1],
                in1=ngB[:, g * 512:(g + 1) * 512], op0=ALU.mult, op1=ALU.mult), reads=[("tA", g), "rstd", "ngB"], writes=[("yn", g)])
        for q in range(4):
            pb_ = q % 2

            def trq(e, q=q, pb_=pb_):
                for j in range(8):
                    kc = q * 8 + j
                    ins = e.transpose(out=ptr[pb_][:, j, :], in_=yn[:, kc * 128:(kc + 1) * 128], identity=idb[:])
                return ins
            P.op("pe", trq, reads=[("yn", 2 * q), ("yn", 2 * q + 1), "idb"], writes=[("ptr", pb_)])
            P.op("act", lambda e, q=q, pb_=pb_: e.copy(out=ytT[:, q * 8:(q + 1) * 8, :], in_=ptr[pb_][:]), reads=[("ptr", pb_)], writes=["ytT"])
        P.op("sp", lambda e: e.dma_start(out=c.YT4[tt], in_=ytT[:].rearrange("p k t -> p (k t)")), reads=["ytT"], dma="ytT")

    load_arow(0)
    loads(0)
    prep(0)
    prep_b(0)
    for tt in range(TT):
        A(tt, 0)
        for g in range(8):
            if g + 1 < 8:
                A(tt, g + 1)
            if g == 6 and tt + 1 < TT:
                load_arow(tt + 1)
            B1(tt, g)
        if tt + 1 < TT:
            loads(tt + 1)
        fin(tt)
        if tt + 1 < TT:
            prep(tt + 1)
        fin_yn(tt)
        if tt + 1 < TT:
            prep_b(tt + 1)
    P.end_phase()


def build(stop_after=None):
    P = Prog()
    c = Ctx()
    nc = P.nc
    ext = lambda name, shape: P.dram(name, shape, F32, kind="ExternalInput").ap()
    x_in = ext("x", [S, D])
    wgu = ext("ffn_w_gate_up", [DEPTH, 2, D, 2 * DFF])
    wdn = ext("ffn_w_down", [DEPTH, 2, DFF, D])
    ln_g = ext("ln_g", [DEPTH, 3, D])
    ln_b = ext("ln_b", [DEPTH, 3, D])
    sb_w_in = ext("sb_w_in", [1, D, 3 * D])
    sb_w_out = ext("sb_w_out", [1, D, D])
    ssd_w_in = ext("ssd_w_in", [1, D, SSD_IN])
    ssd_conv_w = ext("ssd_conv_w", [1, 4, SSD_CONV])
    ssd_conv_b = ext("ssd_conv_b", [1, SSD_CONV])
    ssd_dt_bias = ext("ssd_dt_bias", [1, SSD_H])
    ssd_a_log = ext("ssd_a_log", [1, SSD_H])
    ssd_d = ext("ssd_d", [1, SSD_H])
    ssd_norm_g = ext("ssd_norm_g", [1, SSD_DI])
    ssd_w_out = ext("ssd_w_out", [1, SSD_DI, D])
    c.consts = ext("consts", [128, 2752])
    c.cwl = ext("cwl", [128, 240])
    out = P.dram("out", [S, D], F32, kind="ExternalOutput").ap()
    XA = P.dram("XA", [S, D], F32).ap()
    Y = P.dram("Y", [S, D], F32).ap()
    c.HT = P.dram("HT", [TT, 128, DFF], BF16).ap()
    wpg = nc.sbuf_tensor("wpre", [128, 44 * 512], BF16)
    c.wpre = wpg.__enter__()
    xT_state = {"g": nc.sbuf_tensor("xT", [128, KC, S], BF16)}
    c.xT = xT_state["g"].__enter__()

    steps = []

    xT_n = [0]

    def xT_free():
        xT_state["g"].__exit__(None, None, None)
        xT_state["g"] = None

    def xT_alloc():
        if xT_state["g"] is None:
            xT_n[0] += 1
            xT_state["g"] = nc.sbuf_tensor("xT_%d" % xT_n[0], [128, KC, S], BF16)
            c.xT = xT_state["g"].__enter__()

    def ln(*a, **k):
        xT_alloc()
        ph_ln(P, c, *a, **k)

    def pf_gu(i, j):
        return lambda: gu_load(P, wview(c, "gu"), "wpre", wgu[i, j], 0)

    def pf_dn(wd, nk):
        return lambda: dn_load(P, wview(c, "dn", nk=nk), "wpre", wd, nk, 0)

    def pf_pj(w, col0, nb):
        return lambda: pj_load(P, wview(c, "pj", nb=nb), "wpre", w, col0, 0, nb)

    def ffn(i, j, xres, pre, nxt):
        ph_gate_up(P, c, wgu[i, j], pre=pre, nxt=pf_dn(wdn[i, j], DFF // 128))
        xT_free()
        ph_down(P, c, wdn[i, j], DFF // 128, c.HT, xres, Y, 0.5, pre=True, nxt=nxt)

    n = [0]

    def done():
        n[0] += 1
        return stop_after is not None and n[0] >= stop_after

    def finish(src):
        ph_copy(P, c, src, out)
        if xT_state["g"] is not None:
            xT_free()
        wpg.__exit__(None, None, None)
        P.close()
        return P

    c.QKT = P.dram("QKT", [2 * NH, 128, S], BF16).ap()
    c.QT = c.QKT[0:NH]
    c.KT = c.QKT[NH:2 * NH]
    c.V4 = P.dram("V4", [NH, 128, TT, 128], BF16).ap()
    c.OT4 = P.dram("OT4", [TT, 128, NH * 128], BF16).ap()
    XB = P.dram("XB", [S, D], F32).ap()

    def attn_mixer(xres, nxt):
        w = sb_w_in[0]
        ph_fm(P, c, w, 0, 2 * D, c.QKT, BF16, scale_of_chunk=lambda ch: (HD ** -0.5) if ch < NH else 1.0,
              pre=True, nxt=pf_pj(w, 2 * D, 512))
        V4v = c.V4.rearrange("h p tt d -> p tt h d")

        def store_v(e, tile, fb, tt):
            return e.dma_start(out=V4v[:, tt, fb * 4:(fb + 1) * 4, :], in_=tile[:].rearrange("p (h d) -> p h d", d=128))
        ph_tm(P, c, w, 2 * D, D, store_v, BF16, NS=2, pre=True, nxt=pf_dn(sb_w_out[0], NH))
        xT_free()
        ph_attn(P, c)
        ph_down(P, c, sb_w_out[0], NH, c.OT4, xres, Y, 1.0, pre=True, nxt=nxt)

    full = stop_after is None
    ln(x_in, None, None, None, do_ln=False)
    ffn(0, 0, x_in, False, pf_pj(sb_w_in[0], 0, 512) if full or stop_after >= 3 else None)
    if done():
        return finish(Y)
    ln(Y, XA, ln_g[0, 0:1, :], ln_b[0, 0:1, :])
    if done():
        return finish(XA)
    attn_mixer(XA, pf_gu(0, 1))
    ln(Y, XB, ln_g[0, 1:2, :], ln_b[0, 1:2, :])
    if done():
        return finish(XB)
    ffn(0, 1, XB, True, pf_gu(1, 0))
    ln(Y, XA, ln_g[0, 2:3, :], ln_b[0, 2:3, :])
    if done():
        return finish(XA)
    w = ssd_w_in[0]
    ffn(1, 0, XA, True, pf_pj(w, 0, 512))
    ln(Y, XB, ln_g[1, 0:1, :], ln_b[1, 0:1, :])
    if done():
        return finish(XB)
    c.ZS = P.dram("ZS", [S, SSD_DI], F32).ap()
    c.XBCT = P.dram("XBCT", [48, 128, S], F32).ap()
    c.DTR = P.dram("DTR", [S, SSD_H], F32).ap()
    c.XS = P.dram("XS", [S, SSD_DI], F32).ap()
    c.BT = P.dram("BT", [8, 128, S], BF16).ap()
    c.CT = P.dram("CT", [8, 128, S], BF16).ap()
    c.BTOK = P.dram("BTOK", [S, 1024], BF16).ap()
    c.SM = P.dram("SM", [4, S, SSD_H], F32).ap()
    c.YT4 = P.dram("YT4", [TT, 128, 32 * 128], BF16).ap()
    ph_tm(P, c, w, 0, SSD_DI, lambda e, tile, fb, tt: e.dma_start(
        out=c.ZS[tt * 128:(tt + 1) * 128, fb * 512:(fb + 1) * 512], in_=tile[:]), F32,
        pre=True, nxt=pf_pj(w, SSD_DI + SSD_CONV, SSD_H))
    ph_tm(P, c, w, SSD_DI + SSD_CONV, SSD_H, lambda e, tile, fb, tt: e.dma_start(
        out=c.DTR[tt * 128:(tt + 1) * 128, :], in_=tile[:]), F32, nb=64, NS=2, pre=True, nxt=pf_pj(w, SSD_DI, 512))
    ph_fm(P, c, w, SSD_DI, SSD_CONV, c.XBCT, F32, pre=True, nxt=pf_dn(ssd_w_out[0], SSD_DI // 128))
    xT_free()
    ph_ssd_conv(P, c)
    ph_ssd_dt(P, c, ssd_dt_bias[0:1, :], ssd_a_log[0:1, :])
    ph_ssd_main(P, c, ssd_d[0:1, :], ssd_norm_g[0:1, :])
    ph_down(P, c, ssd_w_out[0], SSD_DI // 128, c.YT4, XB, Y, 1.0, pre=True, nxt=pf_gu(1, 1))
    ln(Y, XA, ln_g[1, 1:2, :], ln_b[1, 1:2, :])
    if done():
        return finish(XA)
    ffn(1, 1, XA, True, None)
    ph_ln(P, c, Y, None, ln_g[1, 2:3, :], ln_b[1, 2:3, :], final_out=out)
    if xT_state["g"] is not None:
        xT_free()
    wpg.__exit__(None, None, None)
    P.close()
    return P


def ph_copy(P, c, src, dst):
    t = [P.sbuf("cp%d" % i, [128, D], F32) for i in range(2)]
    for tt in range(TT):
        i = tt % 2
        rows = slice(tt * 128, (tt + 1) * 128)
        P.op("sp", lambda e, i=i, rows=rows: e.dma_start(out=t[i][:], in_=src[rows, :]), writes=[("t", i)], dma=("t", i))
        P.op("sp", lambda e, i=i, rows=rows: e.dma_start(out=dst[rows, :], in_=t[i][:]), reads=[("t", i)], dma=("t", i))
    P.end_phase()


def make_consts():
    cst = np.zeros((128, 2752), np.float32)
    cst[:, 0:128] = np.eye(128, dtype=np.float32)
    j = np.arange(128)[:, None]
    t = np.arange(512)[None, :]
    for i in range(4):
        cst[:, 128 + i * 512:128 + (i + 1) * 512] = (128 * i + j < t)
    s_ = np.arange(128)[None, :]
    cst[:, 2176:2304] = -(j >= s_).astype(np.float32)
    cst[:, 2304:2432] = -1.0
    k = np.arange(128)[:, None]
    i_ = np.arange(128)[None, :]
    same = (k // 64) == (i_ // 64)
    cst[:, 2432:2560] = (same & (k <= i_))
    cst[:, 2560:2688] = same
    cst[:, 2688:2752] = ((k % 64) <= np.arange(64)[None, :])
    return cst


_CACHE = {}


def run(inputs, stop_after=None, n_cores=8):
    key = stop_after
    if key not in _CACHE:
        _CACHE[key] = build(stop_after)
    P = _CACHE[key]
    names = ["ffn_w_gate_up", "ffn_w_down", "ln_g", "ln_b", "sb_w_in", "sb_w_out", "ssd_w_in", "ssd_conv_w",
             "ssd_conv_b", "ssd_dt_bias", "ssd_a_log", "ssd_d", "ssd_norm_g", "ssd_w_out"]
    shared = {k: np.ascontiguousarray(np.asarray(inputs[k], dtype=np.float32)) for k in names}
    shared["consts"] = make_consts()
    cwb = np.concatenate([shared["ssd_conv_w"][0].T, shared["ssd_conv_b"][0][:, None]], axis=1)
    shared["cwl"] = np.ascontiguousarray(cwb.reshape(48, 128, 5).transpose(1, 0, 2).reshape(128, 240))
    x = np.asarray(inputs["x"], dtype=np.float32)
    in_maps = []
    for b in range(n_cores):
        m = dict(shared)
        m["x"] = np.ascontiguousarray(x[b])
        in_maps.append(m)
    res = run_bass_kernel_spmd(P.nc, in_maps, core_ids=list(range(n_cores)))
    return np.stack([np.asarray(r["out"]) for r in res.results], axis=0)


def kernel(**inputs):
    return run(inputs).astype(np.float32)
```

```python
import math
import numpy as np
import concourse.bass as bass
import concourse.mybir as mybir
from concourse.bass_utils import run_bass_kernel_spmd

F32 = mybir.dt.float32
BF16 = mybir.dt.bfloat16
AF = mybir.ActivationFunctionType
ALU = mybir.AluOpType
AX = mybir.AxisListType

S = 2048
D = 2048
DFF = 5632
DEPTH = 2
NH = 16
HD = 128
TT = S // 128
KC = D // 128
ALPHA = (2.0 * DEPTH) ** 0.25
LN_EPS = 1e-5
RMS_EPS = 1e-5
SSD_DI = 4096
SSD_G = 8
SSD_N = 128
SSD_H = 64
SSD_P = 64
SSD_CONV = SSD_DI + 2 * SSD_G * SSD_N
SSD_IN = SSD_DI + SSD_CONV + SSD_H
CH = 64

ENGS = ("pe", "act", "dve", "pool", "sp")


class Op:
    __slots__ = ("eng", "fn", "deps", "dma", "signal", "event", "waits", "idx")


class Prog:
    def __init__(self, same_engine_sync=True):
        self.nc = bass.Bass("TRN2", target_bir_lowering=False)
        self.same_engine_sync = same_engine_sync
        self._sem_ctx = []
        self.eng_sem = {e: self._mksem("s_" + e) for e in ("pe", "act", "dve", "pool")}
        self.chan_sem = {}
        self.chan_cnt = {}
        self.chan_eng = {}
        self.eng_cnt = {e: 0 for e in ENGS}
        self.seen = {e: {} for e in ENGS}
        self.free_chan = {e: [] for e in ENGS}
        self._uid = 0
        self._reset_phase()
        self.n_instr = 0

    def _mksem(self, name):
        g = self.nc.semaphore(name)
        s = g.__enter__()
        self._sem_ctx.append(g)
        return s

    def _reset_phase(self):
        self.ops = []
        self.lastw = {}
        self.readers = {}
        self._ctx = []
        self.phase_chan = {}

    def sbuf(self, name, shape, dtype):
        self._uid += 1
        g = self.nc.sbuf_tensor("%s_%d" % (name, self._uid), list(shape), dtype)
        t = g.__enter__()
        self._ctx.append(g)
        return t

    def psum(self, name, shape, dtype):
        self._uid += 1
        g = self.nc.psum_tensor("%s_%d" % (name, self._uid), list(shape), dtype)
        t = g.__enter__()
        self._ctx.append(g)
        return t

    def dram(self, name, shape, dtype, kind="Internal"):
        return self.nc.dram_tensor(name, list(shape), dtype, kind=kind)

    def op(self, eng, fn, reads=(), writes=(), dma=None):
        o = Op()
        o.eng, o.fn, o.dma = eng, fn, dma
        o.signal = dma is not None
        o.event = None
        o.idx = len(self.ops)
        deps = set()
        for r in reads:
            w = self.lastw.get(r)
            if w is not None:
                deps.add(w)
        for w_ in writes:
            w = self.lastw.get(w_)
            if w is not None:
                deps.add(w)
            deps.update(self.readers.get(w_, ()))
        deps.discard(o.idx)
        o.deps = deps
        for r in reads:
            self.readers.setdefault(r, []).append(o.idx)
        for w_ in writes:
            self.lastw[w_] = o.idx
            self.readers[w_] = []
        self.ops.append(o)
        return o

    def _skip(self, p, o):
        return p.dma is None and p.eng == o.eng and (p.eng == "pe" or not self.same_engine_sync)

    def end_phase(self):
        nc = self.nc
        ops = self.ops
        last_real = {}
        for o in ops:
            if o.fn is not None and o.dma is None:
                last_real[o.eng] = o.idx
        dma_ops = [o.idx for o in ops if o.dma is not None]
        for e in ENGS:
            o = self.op(e, None)
            o.deps = set(dma_ops) | set(last_real.values())
        for o in ops:
            for d in o.deps:
                p = ops[d]
                if p.dma is None and not self._skip(p, o):
                    p.signal = True
        for o in ops:
            waits = []
            for d in sorted(o.deps):
                p = ops[d]
                if self._skip(p, o):
                    continue
                sem, v = p.event
                k = id(sem)
                if self.seen[o.eng].get(k, 0) >= v:
                    continue
                self.seen[o.eng][k] = v
                waits.append((sem, v))
            o.waits = waits
            if o.dma is not None:
                if o.dma not in self.phase_chan:
                    if self.free_chan[o.eng]:
                        c = self.free_chan[o.eng].pop()
                    else:
                        c = len(self.chan_sem)
                        self.chan_sem[c] = self._mksem("c%d" % c)
                        self.chan_cnt[c] = 0
                    self.phase_chan[o.dma] = c
                    self.chan_eng[c] = o.eng
                c = self.phase_chan[o.dma]
                self.chan_cnt[c] += 16
                o.event = (self.chan_sem[c], self.chan_cnt[c])
            elif o.signal:
                self.eng_cnt[o.eng] += 1
                o.event = (self.eng_sem[o.eng], self.eng_cnt[o.eng])
        per_eng = {e: [o for o in ops if o.eng == e] for e in ENGS}
        self.n_instr += len(ops)

        def emit(e, engobj):
            for o in per_eng[e]:
                for sem, v in o.waits:
                    engobj.wait_ge(sem, v)
                if o.fn is None:
                    continue
                ins = o.fn(engobj)
                if o.event is not None:
                    ins.then_inc(o.event[0], 16 if o.dma is not None else 1)

        with nc.Block() as block:
            @block.tensor
            def _(t):
                emit("pe", t)

            @block.scalar
            def _(t):
                emit("act", t)

            @block.vector
            def _(t):
                emit("dve", t)

            @block.gpsimd
            def _(t):
                emit("pool", t)

            @block.sync
            def _(t):
                emit("sp", t)
        for c_ in self.phase_chan.values():
            self.free_chan[self.chan_eng[c_]].append(c_)
        for g in reversed(self._ctx):
            g.__exit__(None, None, None)
        self._reset_phase()

    def close(self):
        for g in reversed(self._sem_ctx):
            g.__exit__(None, None, None)


class Ctx:
    pass


def ph_ln(P, c, src, dst, g_ap, b_ap, do_ln=True, final_out=None):
    xT = c.xT
    yt = [P.sbuf("ln_y%d" % i, [128, D], F32) for i in range(4)]
    xo = [P.sbuf("ln_o%d" % i, [128, D], F32) for i in range(3)]
    xb = [P.sbuf("ln_b%d" % i, [128, D], BF16) for i in range(2)]
    ident = P.sbuf("ln_id", [128, 128], BF16)
    identf = P.sbuf("ln_idf", [128, 128], F32)
    st = [P.sbuf("ln_st%d" % i, [128, 4, 6], F32) for i in range(3)]
    mv = [P.sbuf("ln_mv%d" % i, [128, 4], F32) for i in range(3)]
    pt = [P.psum("ln_pt%d" % i, [128, 8, 128], BF16) for i in range(4)]
    P.op("sp", lambda e: e.dma_start(out=identf[:], in_=c.consts[0:128, 0:128]), writes=["idf"], dma="idf")
    P.op("dve", lambda e: e.tensor_copy(out=ident[:], in_=identf[:]), reads=["idf"], writes=["id"])
    if do_ln:
        gB = P.sbuf("ln_g", [128, D], F32)
        bB = P.sbuf("ln_bb", [128, D], F32)
        P.op("sp", lambda e: e.dma_start(out=gB[:], in_=g_ap.partition_broadcast(128)), writes=["gB"], dma="gB")
        P.op("sp", lambda e: e.dma_start(out=bB[:], in_=b_ap.partition_broadcast(128)), writes=["bB"], dma="bB")

    def ld(tt):
        y3 = tt % 4
        rows = slice(tt * 128, (tt + 1) * 128)
        P.op("sp", lambda e: e.dma_start(out=yt[y3][:], in_=src[rows, :]), writes=[("yt", y3)], dma=("yt", y3))

    def S1(tt):
        y3, i = tt % 4, tt % 3

        def stats(e):
            for q in range(4):
                ins = e.bn_stats(out=st[i][:, q, :], in_=yt[y3][:, q * 512:(q + 1) * 512])
            return ins
        P.op("dve", stats, reads=[("yt", y3)], writes=[("st", i)])
        P.op("dve", lambda e: e.bn_aggr(out=mv[i][:, 0:2], in_=st[i][:]), reads=[("st", i)], writes=[("mv", i)])
        P.op("dve", lambda e: e.tensor_scalar(out=mv[i][:, 1:2], in0=mv[i][:, 1:2], scalar1=LN_EPS, scalar2=None, op0=ALU.add),
             reads=[("mv", i)], writes=[("mv", i)])
        P.op("act", lambda e: e.activation(out=mv[i][:, 3:4], in_=mv[i][:, 1:2], func=AF.Sqrt), reads=[("mv", i)], writes=[("mv", i)])
        P.op("dve", lambda e: e.reciprocal(out=mv[i][:, 2:3], in_=mv[i][:, 3:4]), reads=[("mv", i)], writes=[("mv", i)])
        P.op("dve", lambda e: e.scalar_tensor_tensor(out=mv[i][:, 3:4], in0=mv[i][:, 0:1], scalar=-1.0, in1=mv[i][:, 2:3],
                                                     op0=ALU.mult, op1=ALU.mult), reads=[("mv", i)], writes=[("mv", i)])

    def norm(tt):
        y3, i = tt % 4, tt % 3
        P.op("act", lambda e: e.activation(out=xo[i][:], in_=yt[y3][:], func=AF.Identity, bias=mv[i][:, 3:4], scale=mv[i][:, 2:3]),
             reads=[("yt", y3), ("mv", i)], writes=[("xo", i)])

    def affine(tt):
        i = tt % 3
        rows = slice(tt * 128, (tt + 1) * 128)
        P.op("dve", lambda e: e.tensor_tensor(out=xo[i][:], in0=xo[i][:], in1=gB[:], op=ALU.mult),
             reads=[("xo", i), "gB"], writes=[("xo", i)])
        P.op("dve", lambda e: e.tensor_tensor(out=xo[i][:], in0=xo[i][:], in1=bB[:], op=ALU.add),
             reads=[("xo", i), "bB"], writes=[("xo", i)])
        dd = dst if final_out is None else final_out
        P.op("sp", lambda e: e.dma_start(out=dd[rows, :], in_=xo[i][:]), reads=[("xo", i)], dma=("xo", i))

    def cast_tr(tt):
        if final_out is not None:
            return
        y3, i, j2 = tt % 4, tt % 3, tt % 2
        srcbuf, srcu = (xo[i], ("xo", i)) if do_ln else (yt[y3], ("yt", y3))
        P.op("act", lambda e: e.copy(out=xb[j2][:], in_=srcbuf[:]), reads=[srcu], writes=[("xb", j2)])
        for hh in range(2):
            pi = (tt * 2 + hh) % 4

            def tr(e, hh=hh, pi=pi):
                for j in range(8):
                    kc = hh * 8 + j
                    ins = e.transpose(out=pt[pi][:, j, :], in_=xb[j2][:, kc * 128:(kc + 1) * 128], identity=ident[:])
                return ins
            P.op("pe", tr, reads=[("xb", j2), "id"], writes=[("pt", pi)])

    def evac(tt):
        if final_out is not None:
            return
        for hh in range(2):
            pi = (tt * 2 + hh) % 4
            if hh == 0:
                f = lambda e, hh=hh, pi=pi: e.copy(out=xT[:, hh * 8:(hh + 1) * 8, tt * 128:(tt + 1) * 128], in_=pt[pi][:])
            else:
                f = lambda e, hh=hh, pi=pi: e.tensor_copy(out=xT[:, hh * 8:(hh + 1) * 8, tt * 128:(tt + 1) * 128], in_=pt[pi][:])
            if do_ln:
                f = lambda e, hh=hh, pi=pi: e.copy(out=xT[:, hh * 8:(hh + 1) * 8, tt * 128:(tt + 1) * 128], in_=pt[pi][:])
            P.op("act" if (hh == 0 or do_ln) else "dve", f, reads=[("pt", pi)], writes=[("xT", tt, hh)])

    ld(0)
    ld(1)
    ld(2)
    if do_ln:
        S1(0)
        for tt in range(TT + 1):
            if tt + 3 < TT:
                ld(tt + 3)
            if tt < TT:
                norm(tt)
            if tt >= 1:
                cast_tr(tt - 1)
            if tt + 1 < TT:
                S1(tt + 1)
            if tt < TT:
                affine(tt)
            if tt >= 1:
                evac(tt - 1)
    else:
        for tt in range(TT):
            if tt + 3 < TT:
                ld(tt + 3)
            cast_tr(tt)
            evac(tt)
    P.end_phase()


def wload(P, dst_ap, src_ap, unit, nsplit=1, kdim=None):
    P.op("pool", lambda e: e.dma_start(out=dst_ap, in_=src_ap), writes=[unit], dma=unit)


def gu_load(P, dst, unit, wgu, fb, FB=512):
    for gu in range(2):
        col0 = gu * DFF + fb * FB
        for half in range(2):
            ks = slice(half * 8, half * 8 + 8)
            src = wgu[half * 1024:(half + 1) * 1024, col0:col0 + FB].rearrange("(kc p) n -> p kc n", p=128)
            P.op("pool", lambda e, gu=gu, ks=ks, src=src: e.dma_start(out=dst[:, gu, ks, :], in_=src),
                 writes=[unit], dma=unit)


def dn_load(P, dst, unit, wd, nk, db, NBd=512):
    kstep = 11 if nk % 11 == 0 else 8
    for k0 in range(0, nk, kstep):
        src = wd[k0 * 128:(k0 + kstep) * 128, db * NBd:(db + 1) * NBd].rearrange("(kc p) n -> p kc n", p=128)
        P.op("pool", lambda e, k0=k0, src=src: e.dma_start(out=dst[:, k0:k0 + kstep, :], in_=src),
             writes=[unit], dma=unit)


def pj_load(P, dst, unit, w, col0, fb, nb):
    for half in range(2):
        ks = slice(half * 8, half * 8 + 8)
        src = w[half * 1024:(half + 1) * 1024, col0 + fb * nb:col0 + (fb + 1) * nb].rearrange("(kc p) n -> p kc n", p=128)
        P.op("pool", lambda e, ks=ks, src=src: e.dma_start(out=dst[:, ks, :], in_=src), writes=[unit], dma=unit)


def wview(c, kind, nk=None, nb=None):
    if kind == "gu":
        return c.wpre[:, 0:2 * KC * 512].rearrange("p (g k n) -> p g k n", g=2, k=KC)
    if kind == "dn":
        return c.wpre[:, 0:nk * 512].rearrange("p (k n) -> p k n", k=nk)
    return c.wpre[:, 0:KC * nb].rearrange("p (k n) -> p k n", k=KC)


def ph_gate_up(P, c, wgu, pre=False, nxt=None):
    xT = c.xT
    NS = 3
    FB = 512
    NB = DFF // FB
    ws = [wview(c, "gu")] + [P.sbuf("gu_w%d" % i, [128, 2, KC, FB], BF16) for i in range(1, NS)]
    wu = ["wpre"] + [("ws", i) for i in range(1, NS)]
    hrow = [P.sbuf("gu_h%d" % i, [128, TT, 4, 128], BF16) for i in range(2)]
    sg = [P.sbuf("gu_s%d" % i, [128, 512], F32) for i in range(1)] * 2
    pg = [P.psum("gu_pg%d" % i, [128, 512], F32) for i in range(2)]
    pu = [P.psum("gu_pu%d" % i, [128, 512], F32) for i in range(2)]
    HTv = c.HT.rearrange("tt p x -> p tt x")
    b0 = max(b_ for b_ in range(NB) if b_ % NS == 0)

    def load(fb):
        if fb == 0 and pre:
            return
        gu_load(P, ws[fb % NS], wu[fb % NS], wgu, fb, FB)

    for fb in range(min(NS, NB)):
        load(fb)
    cnt = 0
    for fb in range(NB):
        sl = fb % NS
        hi = fb % 2
        for fc in range(4):
            for tq in range(4):
                b = cnt % 2
                cnt += 1
                xu = [("xT", tq * 4 + j, hh) for j in range(4) for hh in range(2)]

                def mm(e, sl=sl, fc=fc, tq=tq, b=b):
                    for gu, ps in ((0, pg[b]), (1, pu[b])):
                        for kc in range(KC):
                            ins = e.matmul(ps[:], lhsT=ws[sl][:, gu, kc, fc * 128:(fc + 1) * 128],
                                           rhs=xT[:, kc, tq * 512:(tq + 1) * 512], start=(kc == 0), stop=(kc == KC - 1))
                    return ins
                P.op("pe", mm, reads=[wu[sl]] + xu, writes=[("pg", b), ("pu", b)])
                P.op("act", lambda e, b=b: e.activation(out=sg[b][:], in_=pg[b][:], func=AF.Silu),
                     reads=[("pg", b)], writes=[("sg", 0)])
                P.op("dve", lambda e, b=b, hi=hi, fc=fc, tq=tq: e.tensor_tensor(
                    out=hrow[hi][:, tq * 4:(tq + 1) * 4, fc, :], in0=sg[b][:].rearrange("p (a t) -> p a t", t=128),
                    in1=pu[b][:].rearrange("p (a t) -> p a t", t=128), op=ALU.mult),
                    reads=[("sg", 0), ("pu", b)], writes=[("hrow", hi)])
        P.op("sp", lambda e, hi=hi, fb=fb: e.dma_start(
            out=HTv[:, :, fb * 512:(fb + 1) * 512], in_=hrow[hi][:].rearrange("p tt fc t -> p tt (fc t)")),
            reads=[("hrow", hi)], dma=("hrow", hi))
        if fb + NS < NB:
            load(fb + NS)
        if fb == b0 and nxt is not None:
            nxt()
    P.end_phase()


def ph_down(P, c, wd, nk, act_src, resid, ydst, scale, pre=False, nxt=None):
    NBd = 512
    ws = [wview(c, "dn", nk=nk), P.sbuf("dn_w1", [128, nk, NBd], BF16)]
    wu = ["wpre", ("ws", 1)]
    NHS = 4
    hs = [P.sbuf("dn_h%d" % i, [128, nk * 128], BF16) for i in range(NHS)]
    xr = [P.sbuf("dn_x%d" % i, [128, NBd], F32) for i in range(4)]
    p2 = [P.sbuf("dn_p%d" % i, [128, NBd], F32) for i in range(4)]
    yo = [P.sbuf("dn_y%d" % i, [128, NBd], F32) for i in range(4)]
    py = [P.psum("dn_ps%d" % i, [128, NBd], F32) for i in range(2)]
    NDB = D // NBd
    kstep = 11 if nk % 11 == 0 else 8

    def load(db):
        if db == 0 and pre:
            return
        dn_load(P, ws[db % 2], wu[db % 2], wd, nk, db, NBd)

    PF = 3
    iters = [(db, tt) for db in range(NDB) for tt in range(TT)]

    def loads(n):
        db, tt = iters[n]
        h3, x3 = n % NHS, n % 4
        rows = slice(tt * 128, (tt + 1) * 128)
        cols = slice(db * NBd, (db + 1) * NBd)
        P.op("sp", lambda e, h3=h3, tt=tt: e.dma_start(out=hs[h3][:], in_=act_src[tt]),
             writes=[("hs", h3)], dma=("hs", h3))
        P.op("sp", lambda e, x3=x3, rows=rows, cols=cols: e.dma_start(out=xr[x3][:], in_=resid[rows, cols]),
             writes=[("xr", x3)], dma=("xr", x3))

    load(0)
    for n in range(min(PF, len(iters))):
        loads(n)
    for n, (db, tt) in enumerate(iters):
        if tt == 0 and db + 1 < NDB:
            load(db + 1)
        if tt == 0 and db == NDB - 1 and nxt is not None:
            nxt()
        if n + PF < len(iters):
            loads(n + PF)
        sl = db % 2
        h3, x3, b = n % NHS, n % 4, n % 2
        rows = slice(tt * 128, (tt + 1) * 128)
        cols = slice(db * NBd, (db + 1) * NBd)

        def mm(e, sl=sl, h3=h3, b=b):
            for kc in range(nk):
                ins = e.matmul(py[b][:], lhsT=hs[h3][:, kc * 128:(kc + 1) * 128], rhs=ws[sl][:, kc, :],
                               start=(kc == 0), stop=(kc == nk - 1))
            return ins
        P.op("pe", mm, reads=[wu[sl], ("hs", h3)], writes=[("py", b)])
        P.op("act", lambda e, b=b, x3=x3: e.activation(out=p2[x3][:], in_=py[b][:], func=AF.Copy, scale=float(scale)),
             reads=[("py", b)], writes=[("p2", x3)])
        P.op("dve", lambda e, x3=x3: e.scalar_tensor_tensor(out=yo[x3][:], in0=xr[x3][:], scalar=float(ALPHA),
                                                            in1=p2[x3][:], op0=ALU.mult, op1=ALU.add),
             reads=[("xr", x3), ("p2", x3)], writes=[("yo", x3)])
        P.op("sp", lambda e, x3=x3, rows=rows, cols=cols: e.dma_start(out=ydst[rows, cols], in_=yo[x3][:]),
             reads=[("yo", x3)], dma=("yo", x3))
    P.end_phase()


def ph_fm(P, c, w, col0, ncols, dst, dst_dtype, scale_of_chunk=None, pre=False, nxt=None):
    xT = c.xT
    NS = 3
    FB = 512
    NB = ncols // FB
    ws = [wview(c, "pj", nb=FB)] + [P.sbuf("fm_w%d" % i, [128, KC, FB], BF16) for i in range(1, NS)]
    wu = ["wpre"] + [("ws", i) for i in range(1, NS)]
    row = [P.sbuf("fm_r%d" % i, [128, 4, S], dst_dtype) for i in range(2)]
    ps = [P.psum("fm_p%d" % i, [128, 512], F32) for i in range(4)]
    dstv = dst.rearrange("c p t -> p c t")
    b0 = max(b_ for b_ in range(NB) if b_ % NS == 0)

    def load(fb):
        if fb == 0 and pre:
            return
        pj_load(P, ws[fb % NS], wu[fb % NS], w, col0, fb, FB)

    for fb in range(min(NS, NB)):
        load(fb)
    cnt = 0
    for fb in range(NB):
        sl = fb % NS
        ri = fb % 2
        for fc in range(4):
            sc = 1.0 if scale_of_chunk is None else scale_of_chunk(fb * 4 + fc)
            for tq in range(4):
                b = cnt % 4
                cnt += 1
                xu = [("xT", tq * 4 + j, hh) for j in range(4) for hh in range(2)]

                def mm(e, sl=sl, fc=fc, tq=tq, b=b):
                    for kc in range(KC):
                        ins = e.matmul(ps[b][:], lhsT=ws[sl][:, kc, fc * 128:(fc + 1) * 128],
                                       rhs=xT[:, kc, tq * 512:(tq + 1) * 512], start=(kc == 0), stop=(kc == KC - 1))
                    return ins
                P.op("pe", mm, reads=[wu[sl]] + xu, writes=[("ps", b)])
                if cnt % 2 == 0:
                    P.op("act", lambda e, b=b, ri=ri, fc=fc, tq=tq, sc=sc: e.activation(
                        out=row[ri][:, fc, tq * 512:(tq + 1) * 512], in_=ps[b][:], func=AF.Copy, scale=float(sc)),
                        reads=[("ps", b)], writes=[("row", ri)])
                else:
                    P.op("dve", lambda e, b=b, ri=ri, fc=fc, tq=tq, sc=sc: e.tensor_scalar(
                        out=row[ri][:, fc, tq * 512:(tq + 1) * 512], in0=ps[b][:], scalar1=float(sc), scalar2=None,
                        op0=ALU.mult), reads=[("ps", b)], writes=[("row", ri)])
        P.op("sp", lambda e, ri=ri, fb=fb: e.dma_start(out=dstv[:, fb * 4:(fb + 1) * 4, :], in_=row[ri][:]),
             reads=[("row", ri)], dma=("row", ri))
        if fb + NS < NB:
            load(fb + NS)
        if fb == b0 and nxt is not None:
            nxt()
    P.end_phase()


def ph_tm(P, c, w, col0, ncols, store, dst_dtype, nb=512, NS=3, pre=False, nxt=None):
    xT = c.xT
    NB = ncols // nb
    ws = [wview(c, "pj", nb=nb)] + [P.sbuf("tm_w%d" % i, [128, KC, nb], BF16) for i in range(1, NS)]
    wu = ["wpre"] + [("ws", i) for i in range(1, NS)]
    ot = [P.sbuf("tm_o%d" % i, [128, nb], dst_dtype) for i in range(3)]
    ps = [P.psum("tm_p%d" % i, [128, nb], F32) for i in range(4)]
    b0 = max(b_ for b_ in range(NB) if b_ % NS == 0)

    def load(fb):
        if fb == 0 and pre:
            return
        pj_load(P, ws[fb % NS], wu[fb % NS], w, col0, fb, nb)

    for fb in range(min(NS, NB)):
        load(fb)
    cnt = 0
    for fb in range(NB):
        sl = fb % NS
        for tt in range(TT):
            b = cnt % 4
            o3 = cnt % 3
            cnt += 1

            def mm(e, sl=sl, tt=tt, b=b):
                for kc in range(KC):
                    ins = e.matmul(ps[b][:], lhsT=xT[:, kc, tt * 128:(tt + 1) * 128], rhs=ws[sl][:, kc, :],
                                   start=(kc == 0), stop=(kc == KC - 1))
                return ins
            P.op("pe", mm, reads=[wu[sl], ("xT", tt, 0), ("xT", tt, 1)], writes=[("ps", b)])
            if cnt % 2 == 0:
                P.op("act", lambda e, b=b, o3=o3: e.copy(out=ot[o3][:], in_=ps[b][:]), reads=[("ps", b)], writes=[("ot", o3)])
            else:
                P.op("dve", lambda e, b=b, o3=o3: e.tensor_copy(out=ot[o3][:], in_=ps[b][:]), reads=[("ps", b)], writes=[("ot", o3)])
            P.op("sp", lambda e, o3=o3, fb=fb, tt=tt: store(e, ot[o3], fb, tt), reads=[("ot", o3)], dma=("ot", o3))
        if fb + NS < NB:
            load(fb + NS)
        if fb == b0 and nxt is not None:
            nxt()
    P.end_phase()


def ph_attn(P, c):
    cf = P.sbuf("at_cf", [128, 2304], F32)
    cb = P.sbuf("at_cb", [128, 2304], BF16)
    P.op("sp", lambda e: e.dma_start(out=cf[:], in_=c.consts[:, 128:2432]), writes=["cf"], dma="cf")
    P.op("dve", lambda e: e.tensor_copy(out=cb[:], in_=cf[:]), reads=["cf"], writes=["cb"])
    mask = lambda i: cb[:, i * 512:(i + 1) * 512]
    negtri = cb[:, 2048:2176]
    negone = cb[:, 2176:2304]
    qT = [P.sbuf("at_q%d" % i, [128, S], BF16) for i in range(2)]
    kT = [P.sbuf("at_k%d" % i, [128, S], BF16) for i in range(2)]
    vv = [P.sbuf("at_v%d" % i, [128, 16, 128], BF16) for i in range(2)]
    orow = [P.sbuf("at_o%d" % i, [128, TT, 128], BF16) for i in range(2)]
    ef = [P.sbuf("at_e%d" % i, [128, 512], F32) for i in range(3)]
    pb = [P.sbuf("at_pb%d" % i, [128, 512], BF16) for i in range(3)]
    rs = P.sbuf("at_rs", [128, 512], F32)
    rsb = [P.sbuf("at_rsb%d" % i, [128, 512], BF16) for i in range(3)]
    att = [P.sbuf("at_a%d" % i, [128, 512], BF16) for i in range(3)]
    pA = [P.psum("at_pA%d" % i, [128, 512], F32) for i in range(2)]
    pB = [P.psum("at_pB%d" % i, [128, 512], F32) for i in range(2)]
    pO = [P.psum("at_pO%d" % i, [128, 512], F32) for i in range(2)]
    OTv = c.OT4.rearrange("tt p (h t) -> p tt h t", h=NH)

    def loadh(h):
        i = h % 2
        P.op("sp", lambda e: e.dma_start(out=qT[i][:], in_=c.QT[h]), writes=[("q", i)], dma=("q", i))
        P.op("sp", lambda e: e.dma_start(out=kT[i][:], in_=c.KT[h]), writes=[("k", i)], dma=("k", i))
        P.op("sp", lambda e: e.dma_start(out=vv[i][:], in_=c.V4[h]), writes=[("v", i)], dma=("v", i))

    blocks = [(h, cq, kb) for h in range(NH) for cq in range(4) for kb in range(4 * cq + 3, -1, -1)]
    nb = len(blocks)

    def issue_A(j):
        h, cq, kb = blocks[j]
        hi, a = h % 2, j % 2
        P.op("pe", lambda e: e.matmul(
            pA[a][:], lhsT=kT[hi][:, kb * 128:(kb + 1) * 128], rhs=qT[hi][:, cq * 512:(cq + 1) * 512],
            start=True, stop=True), reads=[("q", hi), ("k", hi)], writes=[("pA", a)])

    def act_e(j):
        a, p3 = j % 2, j % 3
        P.op("act", lambda e: e.activation(out=ef[p3][:], in_=pA[a][:], func=AF.Exp), reads=[("pA", a)], writes=[("ef", p3)])

    def act_P(j):
        h, cq, kb = blocks[j]
        p3 = j % 3
        first = kb == 4 * cq + 3
        diag = kb >= 4 * cq
        P.op("act", lambda e: e.activation(out=pb[p3][:], in_=ef[p3][:], func=AF.Ln, bias=1.0), reads=[("ef", p3)], writes=[("pb", p3)])
        if diag:
            P.op("dve", lambda e: e.tensor_tensor(out=pb[p3][:], in0=pb[p3][:], in1=mask(kb - 4 * cq), op=ALU.mult),
                 reads=[("pb", p3), "cb"], writes=[("pb", p3)])
        if kb > 0:
            if first:
                P.op("dve", lambda e: e.tensor_copy(out=rs[:], in_=pb[p3][:]), reads=[("pb", p3)], writes=["rs"])
            else:
                P.op("dve", lambda e: e.tensor_tensor(out=rs[:], in0=rs[:], in1=pb[p3][:], op=ALU.add), reads=[("pb", p3), "rs"], writes=["rs"])
            P.op("dve", lambda e: e.tensor_copy(out=rsb[p3][:], in_=rs[:]), reads=["rs"], writes=[("rsb", p3)])

    def issue_B(j):
        h, cq, kb = blocks[j]
        hi, b, p3 = h % 2, j % 2, j % 3
        first = kb == 4 * cq + 3

        def f(e):
            e.matmul(pB[b][:], lhsT=kT[hi][:, kb * 128:(kb + 1) * 128], rhs=qT[hi][:, cq * 512:(cq + 1) * 512],
                     start=True, stop=False)
            ins = e.matmul(pB[b][:], lhsT=negtri, rhs=pb[p3][:], start=False, stop=first)
            if not first:
                ins = e.matmul(pB[b][:], lhsT=negone, rhs=rsb[(j - 1) % 3][:], start=False, stop=True)
            return ins
        rd = [("q", hi), ("k", hi), ("pb", p3), "cb"] + ([] if first else [("rsb", (j - 1) % 3)])
        P.op("pe", f, reads=rd, writes=[("pB", b)])

    def act_att(j):
        h, cq, kb = blocks[j]
        b, a3 = j % 2, j % 3
        P.op("act", lambda e: e.activation(out=att[a3][:], in_=pB[b][:], func=AF.Exp), reads=[("pB", b)], writes=[("att", a3)])
        if kb >= 4 * cq:
            P.op("dve", lambda e: e.tensor_tensor(out=att[a3][:], in0=att[a3][:], in1=mask(kb - 4 * cq), op=ALU.mult),
                 reads=[("att", a3), "cb"], writes=[("att", a3)])

    def issue_AV(j):
        h, cq, kb = blocks[j]
        hi, a3, o = h % 2, j % 3, cq % 2
        first = kb == 4 * cq + 3
        P.op("pe", lambda e: e.matmul(pO[o][:], lhsT=vv[hi][:, kb, :], rhs=att[a3][:], start=first, stop=(kb == 0)),
             reads=[("v", hi), ("att", a3)], writes=[("pO", o)])
        if kb == 0:
            P.op("dve", lambda e: e.tensor_copy(out=orow[hi][:, cq * 4:(cq + 1) * 4, :],
                                                in_=pO[o][:].rearrange("p (a t) -> p a t", t=128)),
                 reads=[("pO", o)], writes=[("orow", hi)])
            if cq == 3:
                if h + 2 < NH:
                    loadh(h + 2)
                P.op("sp", lambda e: e.dma_start(out=OTv[:, :, h, :], in_=orow[hi][:]), reads=[("orow", hi)], dma=("orow", hi))

    loadh(0)
    loadh(1)
    issue_A(0)
    issue_A(1)
    act_e(0)
    act_e(1)
    issue_A(2)
    act_P(0)
    for j in range(nb):
        if j + 3 < nb:
            issue_A(j + 3)
        if j + 2 < nb:
            act_e(j + 2)
        if j + 1 < nb:
            act_P(j + 1)
        issue_B(j)
        if j >= 1:
            issue_AV(j - 1)
        act_att(j)
    issue_AV(nb - 1)
    P.end_phase()


def ph_ssd_conv(P, c):
    cw = P.sbuf("cv_w", [128, 48, 5], F32)
    idf = P.sbuf("cv_idf", [128, 128], F32)
    idb = P.sbuf("cv_idb", [128, 128], BF16)
    buf = [P.sbuf("cv_x%d" % i, [128, 3 + S], F32) for i in range(3)]
    acc = [P.sbuf("cv_a%d" % i, [128, S], F32) for i in range(3)]
    so = [P.sbuf("cv_s%d" % i, [128, S], F32) for i in range(3)]
    sb = [P.sbuf("cv_b%d" % i, [128, S], BF16) for i in range(3)]
    stg = [P.sbuf("cv_g%d" % i, [128, TT, 128], F32) for i in range(3)]
    stgb = [P.sbuf("cv_h%d" % i, [128, TT, 128], BF16) for i in range(3)]
    pt = [P.psum("cv_p%d" % i, [128, 4, 128], F32) for i in range(4)]
    ptb = [P.psum("cv_q%d" % i, [128, 8, 128], BF16) for i in range(2)]
    P.op("sp", lambda e: e.dma_start(out=cw[:].rearrange("p a b -> p (a b)"), in_=c.cwl), writes=["cw"], dma="cw")
    P.op("sp", lambda e: e.dma_start(out=idf[:], in_=c.consts[:, 0:128]), writes=["idf"], dma="idf")
    P.op("dve", lambda e: e.tensor_copy(out=idb[:], in_=idf[:]), reads=["idf"], writes=["idb"])
    for i in range(3):
        P.op("dve", lambda e, i=i: e.memset(buf[i][:, 0:3], 0.0), writes=[("buf", i)])
    XSv = c.XS.rearrange("(tt p) c -> p tt c", p=128)
    BTKv = c.BTOK.rearrange("(tt p) c -> p tt c", p=128)
    def ldc(ch):
        i = ch % 3
        P.op("sp", lambda e, i=i, ch=ch: e.dma_start(out=buf[i][:, 3:3 + S], in_=c.XBCT[ch]), writes=[("buf", i)], dma=("buf", i))

    def SA(ch):
        i = ch % 3
        P.op("act", lambda e: e.activation(out=acc[i][:], in_=buf[i][:, 0:S], func=AF.Identity,
                                           bias=cw[:, ch, 4:5], scale=cw[:, ch, 0:1]),
             reads=[("buf", i), "cw"], writes=[("acc", i)])
        for k in range(1, 4):
            P.op("dve", lambda e, k=k: e.scalar_tensor_tensor(
                out=acc[i][:], in0=buf[i][:, k:k + S], scalar=cw[:, ch, k:k + 1], in1=acc[i][:], op0=ALU.mult, op1=ALU.add),
                reads=[("buf", i), ("acc", i), "cw"], writes=[("acc", i)])

    def SB(ch):
        i = ch % 3
        if ch < 32:
            P.op("act", lambda e: e.activation(out=so[i][:], in_=acc[i][:], func=AF.Silu), reads=[("acc", i)], writes=[("so", i)])
            for q in range(4):
                def tr(e, q=q):
                    for j in range(4):
                        tt = q * 4 + j
                        ins = e.transpose(out=pt[q][:, j, :], in_=so[i][:, tt * 128:(tt + 1) * 128], identity=idf[:])
                    return ins
                P.op("pe", tr, reads=[("so", i), "idf"], writes=[("pt", q)])
                P.op("act", lambda e, q=q: e.copy(out=stg[i][:, q * 4:(q + 1) * 4, :], in_=pt[q][:]), reads=[("pt", q)], writes=[("stg", i)])
            P.op("sp", lambda e: e.dma_start(out=XSv[:, :, ch * 128:(ch + 1) * 128], in_=stg[i][:]), reads=[("stg", i)], dma=("stg", i))
        else:
            g = (ch - 32) % 8
            P.op("act", lambda e: e.activation(out=sb[i][:], in_=acc[i][:], func=AF.Silu), reads=[("acc", i)], writes=[("sb", i)])
            dstT = c.BT if ch < 40 else c.CT
            P.op("sp", lambda e: e.dma_start(out=dstT[g], in_=sb[i][:]), reads=[("sb", i)], dma=("sb", i))
            if ch < 40:
                for q in range(2):
                    def trb(e, q=q):
                        for j in range(8):
                            tt = q * 8 + j
                            ins = e.transpose(out=ptb[q][:, j, :], in_=sb[i][:, tt * 128:(tt + 1) * 128], identity=idb[:])
                        return ins
                    P.op("pe", trb, reads=[("sb", i), "idb"], writes=[("ptb", q)])
                    P.op("act", lambda e, q=q: e.copy(out=stgb[i][:, q * 8:(q + 1) * 8, :], in_=ptb[q][:]), reads=[("ptb", q)], writes=[("stgb", i)])
                P.op("sp", lambda e: e.dma_start(out=BTKv[:, :, g * 128:(g + 1) * 128], in_=stgb[i][:]), reads=[("stgb", i)], dma=("stgb", i))

    ldc(0)
    ldc(1)
    SA(0)
    for ch in range(48):
        if ch + 2 < 48:
            ldc(ch + 2)
        if ch + 1 < 48:
            SA(ch + 1)
        SB(ch)
    P.end_phase()


def ph_ssd_dt(P, c, dt_bias, a_log):
    dbias = P.sbuf("dt_b", [128, 64], F32)
    aB = P.sbuf("dt_a", [128, 64], F32)
    U2 = P.sbuf("dt_u", [128, 256], F32)
    P.op("sp", lambda e: e.dma_start(out=dbias[:], in_=dt_bias.partition_broadcast(128)), writes=["dbias"], dma="dbias")
    P.op("sp", lambda e: e.dma_start(out=aB[:], in_=a_log.partition_broadcast(128)), writes=["aB"], dma="aB")
    P.op("sp", lambda e: e.dma_start(out=U2[:], in_=c.consts[:, 2432:2688]), writes=["U2"], dma="U2")
    P.op("act", lambda e: e.activation(out=aB[:], in_=aB[:], func=AF.Exp), reads=["aB"], writes=["aB"])
    P.op("dve", lambda e: e.tensor_scalar(out=aB[:], in0=aB[:], scalar1=-1.0, scalar2=None, op0=ALU.mult), reads=["aB"], writes=["aB"])
    t = {k: [P.sbuf("dt_%s%d" % (k, i), [128, 64], F32) for i in range(2)] for k in ("r", "x", "e", "dt", "dta", "ac", "dd", "dte", "wdt", "E")}
    pac = [P.psum("dt_pa%d" % i, [128, 64], F32) for i in range(2)]
    pla = [P.psum("dt_pl%d" % i, [128, 64], F32) for i in range(2)]
    for tt in range(TT):
        i = tt % 2
        rows = slice(tt * 128, (tt + 1) * 128)
        u = lambda k, i=i: (k, i)
        P.op("sp", lambda e, i=i, rows=rows: e.dma_start(out=t["r"][i][:], in_=c.DTR[rows, :]), writes=[u("r")], dma=u("r"))
        P.op("dve", lambda e, i=i: e.tensor_tensor(out=t["x"][i][:], in0=t["r"][i][:], in1=dbias[:], op=ALU.add), reads=[u("r"), "dbias"], writes=[u("x")])
        P.op("act", lambda e, i=i: e.activation(out=t["e"][i][:], in_=t["x"][i][:], func=AF.Exp), reads=[u("x")], writes=[u("e")])
        P.op("act", lambda e, i=i: e.activation(out=t["dt"][i][:], in_=t["e"][i][:], func=AF.Ln, bias=1.0), reads=[u("e")], writes=[u("dt")])
        P.op("dve", lambda e, i=i: e.tensor_tensor(out=t["dta"][i][:], in0=t["dt"][i][:], in1=aB[:], op=ALU.mult), reads=[u("dt"), "aB"], writes=[u("dta")])
        P.op("pe", lambda e, i=i: e.matmul(pac[i][:], lhsT=U2[:, 0:128], rhs=t["dta"][i][:], start=True, stop=True), reads=[u("dta"), "U2"], writes=[u("pac")])
        P.op("pe", lambda e, i=i: e.matmul(pla[i][:], lhsT=U2[:, 128:256], rhs=t["dta"][i][:], start=True, stop=True), reads=[u("dta"), "U2"], writes=[u("pla")])
        P.op("act", lambda e, i=i: e.copy(out=t["ac"][i][:], in_=pac[i][:]), reads=[u("pac")], writes=[u("ac")])
        P.op("dve", lambda e, i=i: e.tensor_tensor(out=t["dd"][i][:], in0=pla[i][:], in1=t["ac"][i][:], op=ALU.subtract), reads=[u("pla"), u("ac")], writes=[u("dd")])
        P.op("act", lambda e, i=i: e.activation(out=t["dte"][i][:], in_=t["dd"][i][:], func=AF.Exp), reads=[u("dd")], writes=[u("dte")])
        P.op("dve", lambda e, i=i: e.tensor_tensor(out=t["wdt"][i][:], in0=t["dt"][i][:], in1=t["dte"][i][:], op=ALU.mult), reads=[u("dt"), u("dte")], writes=[u("wdt")])
        P.op("act", lambda e, i=i: e.activation(out=t["E"][i][:], in_=t["ac"][i][:], func=AF.Exp), reads=[u("ac")], writes=[u("E")])
        for k, dstd in (("dt", 0), ("wdt", 1), ("E", 2), ("ac", 3)):
            P.op("sp", lambda e, i=i, k=k, dstd=dstd, rows=rows: e.dma_start(out=c.SM[dstd, rows, :], in_=t[k][i][:]), reads=[u(k)], dma=u(k))
    P.end_phase()


def ph_ssd_main(P, c, ssd_d, ssd_norm_g):
    H = SSD_H
    mk = P.sbuf("sm_mk", [128, 64], F32)
    idf = P.sbuf("sm_idf", [128, 128], F32)
    idb = P.sbuf("sm_idb", [128, 128], BF16)
    dsk = P.sbuf("sm_dsk", [128, H], F32)
    ngB = P.sbuf("sm_ng", [128, SSD_DI], F32)
    hst = P.sbuf("sm_hst", [128, SSD_DI], F32)
    hb = P.sbuf("sm_hb", [128, SSD_DI], BF16)
    M2 = P.sbuf("sm_M2", [128, 128, H], BF16)
    ctp = P.sbuf("sm_ctp", [128, 8, 2, 128], BF16)
    bpd = P.sbuf("sm_bpd", [128, 2, 1024], BF16)
    xs = P.sbuf("sm_xs", [128, H, 64], F32)
    ztg = [P.sbuf("sm_zt%d" % i, [128, 512], F32) for i in range(2)]
    bt = P.sbuf("sm_bt", [128, 8, 128], BF16)
    ct = P.sbuf("sm_ct", [128, 8, 128], BF16)
    btok = P.sbuf("sm_btok", [128, 1024], BF16)
    sm = [P.sbuf("sm_sm%d" % i, [128, 4, H], F32) for i in range(2)]
    arow = P.sbuf("sm_arow", [128, 64, H], F32)
    cdr = [P.sbuf("sm_cdr%d" % i, [128, 2, H], F32) for i in range(2)]
    xdt = P.sbuf("sm_xdt", [128, H, 64], BF16)
    xw = P.sbuf("sm_xw", [128, H, 64], BF16)
    cbm = P.sbuf("sm_cbm", [128, 64, 8], F32)
    tA = [P.sbuf("sm_tA%d" % i, [128, 8, 64], F32) for i in range(8)]
    tB = P.sbuf("sm_tB", [128, 8, 64], F32)
    junk = P.sbuf("sm_junk", [128, 512], BF16)
    ssq = P.sbuf("sm_ssq", [128, 8], F32)
    rstd = P.sbuf("sm_rstd", [128, 8], F32)
    yn = P.sbuf("sm_yn", [128, SSD_DI], BF16)
    ytT = P.sbuf("sm_ytT", [128, 32, 128], BF16)
    pcb = P.psum("sm_pcb", [128, 128], F32)
    pyd = [P.psum("sm_pyd%d" % i, [128, 8, 64], F32) for i in range(2)]
    pyo = [P.psum("sm_pyo%d" % i, [128, 8, 64], F32) for i in range(2)]
    pst = P.psum("sm_pst", [128, 8, 64], F32)
    ptr = [P.psum("sm_ptr%d" % i, [128, 8, 128], BF16) for i in range(2)]
    P.op("sp", lambda e: e.dma_start(out=mk[:], in_=c.consts[:, 2688:2752]), writes=["mk"], dma="mk")
    P.op("sp", lambda e: e.dma_start(out=idf[:], in_=c.consts[:, 0:128]), writes=["idf"], dma="idf")
    P.op("dve", lambda e: e.tensor_copy(out=idb[:], in_=idf[:]), reads=["idf"], writes=["idb"])
    P.op("sp", lambda e: e.dma_start(out=dsk[:], in_=ssd_d.partition_broadcast(128)), writes=["dsk"], dma="dsk")
    P.op("sp", lambda e: e.dma_start(out=ngB[:], in_=ssd_norm_g.partition_broadcast(128)), writes=["ngB"], dma="ngB")
    P.op("dve", lambda e: e.memset(hst[:], 0.0), writes=["hst"])
    P.op("dve", lambda e: e.memset(hb[:], 0.0), writes=["hb"])
    P.op("dve", lambda e: e.memset(M2[:], 0.0), writes=["M2"])
    P.op("dve", lambda e: e.memset(ctp[:], 0.0), writes=["ctp"])
    P.op("dve", lambda e: e.memset(bpd[:], 0.0), writes=["bpd"])
    ACc = c.SM[3].rearrange("(cc i) h -> cc i h", i=64)
    bc = lambda ap: ap.unsqueeze(2).to_broadcast([128, H, 64])

    def zload(tt, g):
        zb = g % 2
        rows = slice(tt * 128, (tt + 1) * 128)
        P.op("sp", lambda e: e.dma_start(out=ztg[zb][:], in_=c.ZS[rows, g * 512:(g + 1) * 512]), writes=[("zt", zb)], dma=("zt", zb))

    def load_arow(tt):
        for cc in range(2):
            P.op("sp", lambda e, cc=cc: e.dma_start(
                out=arow[cc * 64:(cc + 1) * 64].rearrange("p i h -> p (i h)"),
                in_=ACc[2 * tt + cc:2 * tt + cc + 1].rearrange("a i h -> a (i h)").partition_broadcast(64)), writes=["arow"], dma="arow")

    def loads(tt):
        rows = slice(tt * 128, (tt + 1) * 128)
        P.op("sp", lambda e: e.dma_start(out=sm[tt % 2][:], in_=c.SM[:, rows, :].rearrange("k p h -> p k h")), writes=[("sm", tt % 2)], dma=("sm", tt % 2))
        P.op("sp", lambda e: e.dma_start(out=xs[:].rearrange("p h d -> p (h d)"), in_=c.XS[rows, :]), writes=["xs"], dma="xs")
        P.op("sp", lambda e: e.dma_start(out=bt[:], in_=c.BT[:, :, rows].rearrange("g n t -> n g t")), writes=["bt"], dma="bt")
        P.op("sp", lambda e: e.dma_start(out=ct[:], in_=c.CT[:, :, rows].rearrange("g n t -> n g t")), writes=["ct"], dma="ct")
        P.op("sp", lambda e: e.dma_start(out=btok[:], in_=c.BTOK[rows, :]), writes=["btok"], dma="btok")
        for cc in range(2):
            P.op("sp", lambda e, cc=cc: e.dma_start(out=cdr[tt % 2][:, cc, :], in_=ACc[2 * tt + cc, 63:64, :].partition_broadcast(128)),
                 writes=[("cdr", tt % 2)], dma=("cdr", tt % 2))
        zload(tt, 0)

    def prep(tt):
        P.op("dve", lambda e: e.tensor_tensor(out=arow[:], in0=arow[:], in1=sm[tt % 2][:, 3, :].unsqueeze(1).to_broadcast([128, 64, H]), op=ALU.subtract),
             reads=["arow", ("sm", tt % 2)], writes=["arow"])
        P.op("dve", lambda e: e.tensor_scalar(out=arow[:], in0=arow[:], scalar1=0.0, scalar2=None, op0=ALU.min), reads=["arow"], writes=["arow"])
        P.op("act", lambda e: e.activation(out=arow[:], in_=arow[:], func=AF.Exp), reads=["arow"], writes=["arow"])
        P.op("act", lambda e: e.activation(out=cdr[tt % 2][:], in_=cdr[tt % 2][:], func=AF.Exp), reads=[("cdr", tt % 2)], writes=[("cdr", tt % 2)])

    def prep_b(tt):
        P.op("dve", lambda e: e.tensor_tensor(out=xdt[:], in0=xs[:], in1=bc(sm[tt % 2][:, 0, :]), op=ALU.mult), reads=["xs", ("sm", tt % 2)], writes=["xdt"])
        P.op("dve", lambda e: e.tensor_tensor(out=xw[:], in0=xs[:], in1=bc(sm[tt % 2][:, 1, :]), op=ALU.mult), reads=["xs", ("sm", tt % 2)], writes=["xw"])
        for cc in range(2):
            ps_ = slice(cc * 64, (cc + 1) * 64)
            P.op("act", lambda e, cc=cc, ps_=ps_: e.copy(out=ctp[:, :, cc, ps_], in_=ct[:, :, ps_]), reads=["ct"], writes=["ctp"])
            P.op("act", lambda e, cc=cc, ps_=ps_: e.copy(out=bpd[ps_, cc, :], in_=btok[ps_, :]), reads=["btok"], writes=["bpd"])

    def A(tt, g):
        gs = slice(g * 8, (g + 1) * 8)
        b = g % 2
        P.op("pe", lambda e: e.matmul(pcb[:], lhsT=bt[:, g, :], rhs=ct[:, g, :], start=True, stop=True), reads=["bt", "ct"], writes=["pcb"])
        for cc in range(2):
            ps_ = slice(cc * 64, (cc + 1) * 64)
            P.op("dve", lambda e, ps_=ps_: e.tensor_tensor(out=cbm[ps_, :, g], in0=pcb[ps_, ps_], in1=mk[ps_, :], op=ALU.mult),
                 reads=["pcb", "mk"], writes=["cbm"])
        for cc in range(2):
            ps_ = slice(cc * 64, (cc + 1) * 64)
            P.op("dve", lambda e, ps_=ps_: e.tensor_tensor(
                out=M2[ps_, ps_, gs], in0=arow[ps_, :, gs], in1=cbm[ps_, :, g:g + 1].to_broadcast([64, 64, 8]), op=ALU.mult),
                reads=["arow", "cbm"], writes=[("M2", g)])

        def ydiag(e):
            for r in range(8):
                h = g * 8 + r
                ins = e.matmul(pyd[b][:, r, :], lhsT=M2[:, :, h], rhs=xdt[:, h, :], start=True, stop=True)
            return ins
        P.op("pe", ydiag, reads=[("M2", g), "xdt"], writes=[("pyd", b)])
        hv = hst[:, g * 512:(g + 1) * 512].rearrange("p (r d) -> p r d", d=64)
        for cc in range(2):
            P.op("pe", lambda e, cc=cc: e.matmul(
                pyo[b][:].rearrange("p r d -> p (r d)"), lhsT=ctp[:, g, cc, :], rhs=hb[:, g * 512:(g + 1) * 512],
                start=(cc == 0), stop=(cc == 1)), reads=["ctp", ("hb", g)], writes=[("pyo", b)])
            P.op("pe", lambda e, cc=cc: e.matmul(
                pst[:].rearrange("p r d -> p (r d)"), lhsT=bpd[:, cc, g * 128:(g + 1) * 128],
                rhs=xw[:, g * 8:(g + 1) * 8, :].rearrange("p r d -> p (r d)"),
                start=True, stop=True), reads=["bpd", "xw"], writes=["pst"])
            P.op("dve", lambda e, cc=cc: e.tensor_tensor(
                out=hv, in0=hv, in1=cdr[tt % 2][:, cc, g * 8:(g + 1) * 8].unsqueeze(2).to_broadcast([128, 8, 64]), op=ALU.mult),
                reads=[("hst", g), ("cdr", tt % 2)], writes=[("hst", g)])
            P.op("dve", lambda e: e.tensor_tensor(out=hv, in0=hv, in1=pst[:], op=ALU.add),
                 reads=[("hst", g), "pst"], writes=[("hst", g)])
            P.op("act", lambda e: e.copy(out=hb[:, g * 512:(g + 1) * 512], in_=hst[:, g * 512:(g + 1) * 512]),
                 reads=[("hst", g)], writes=[("hb", g)])

    def B1(tt, g):
        gs = slice(g * 8, (g + 1) * 8)
        b, zb = g % 2, g % 2
        if g + 1 < 8:
            zload(tt, g + 1)
        Eb = sm[tt % 2][:, 2, g * 8:(g + 1) * 8].unsqueeze(2).to_broadcast([128, 8, 64])
        Db = dsk[:, g * 8:(g + 1) * 8].unsqueeze(2).to_broadcast([128, 8, 64])
        tg = tA[g]
        tgf = tg[:].rearrange("p r d -> p (r d)")
        u = ("tA", g)
        P.op("act", lambda e: e.activation(out=ztg[zb][:], in_=ztg[zb][:], func=AF.Silu), reads=[("zt", zb)], writes=[("zt", zb)])
        P.op("dve", lambda e: e.tensor_tensor(out=tg[:], in0=pyo[b][:], in1=Eb, op=ALU.mult), reads=[("pyo", b), ("sm", tt % 2)], writes=[u])
        P.op("dve", lambda e: e.tensor_tensor(out=tg[:], in0=tg[:], in1=pyd[b][:], op=ALU.add), reads=[u, ("pyd", b)], writes=[u])
        P.op("dve", lambda e: e.tensor_tensor(out=tB[:], in0=xs[:, gs, :], in1=Db, op=ALU.mult), reads=["xs", "dsk"], writes=["tB"])
        P.op("dve", lambda e: e.tensor_tensor(out=tg[:], in0=tg[:], in1=tB[:], op=ALU.add), reads=[u, "tB"], writes=[u])
        P.op("dve", lambda e: e.tensor_tensor(out=tgf, in0=tgf, in1=ztg[zb][:], op=ALU.mult), reads=[u, ("zt", zb)], writes=[u])
        P.op("act", lambda e: e.activation(out=junk[:], in_=tgf, func=AF.Square, accum_out=ssq[:, g:g + 1]),
             reads=[u], writes=["junk", ("ssq", g)])

    def fin(tt):
        su = [("ssq", g) for g in range(8)]
        P.op("act", lambda e: e.activation(out=rstd[:], in_=ssq[:], func=AF.Ln, scale=1.0 / 512.0, bias=RMS_EPS), reads=su, writes=["rstd"])
        P.op("act", lambda e: e.activation(out=rstd[:], in_=rstd[:], func=AF.Exp, scale=-0.5), reads=["rstd"], writes=["rstd"])

    def fin_yn(tt):
        for g in range(8):
            P.op("dve", lambda e, g=g: e.scalar_tensor_tensor(
                out=yn[:, g * 512:(g + 1) * 512], in0=tA[g][:].rearrange("p r d -> p (r d)"), scalar=rstd[:, g:g + 1],
                in1=ngB[:, g * 512:(g + 1) * 512], op0=ALU.mult, op1=ALU.mult), reads=[("tA", g), "rstd", "ngB"], writes=[("yn", g)])
        for q in range(4):
            pb_ = q % 2

            def trq(e, q=q, pb_=pb_):
                for j in range(8):
                    kc = q * 8 + j
                    ins = e.transpose(out=ptr[pb_][:, j, :], in_=yn[:, kc * 128:(kc + 1) * 128], identity=idb[:])
                return ins
            P.op("pe", trq, reads=[("yn", 2 * q), ("yn", 2 * q + 1), "idb"], writes=[("ptr", pb_)])
            P.op("act", lambda e, q=q, pb_=pb_: e.copy(out=ytT[:, q * 8:(q + 1) * 8, :], in_=ptr[pb_][:]), reads=[("ptr", pb_)], writes=["ytT"])
        P.op("sp", lambda e: e.dma_start(out=c.YT4[tt], in_=ytT[:].rearrange("p k t -> p (k t)")), reads=["ytT"], dma="ytT")

    load_arow(0)
    loads(0)
    prep(0)
    prep_b(0)
    for tt in range(TT):
        A(tt, 0)
        for g in range(8):
            if g + 1 < 8:
                A(tt, g + 1)
            if g == 6 and tt + 1 < TT:
                load_arow(tt + 1)
            B1(tt, g)
        if tt + 1 < TT:
            loads(tt + 1)
        fin(tt)
        if tt + 1 < TT:
            prep(tt + 1)
        fin_yn(tt)
        if tt + 1 < TT:
            prep_b(tt + 1)
    P.end_phase()


def build(stop_after=None):
    P = Prog()
    c = Ctx()
    nc = P.nc
    ext = lambda name, shape: P.dram(name, shape, F32, kind="ExternalInput").ap()
    x_in = ext("x", [S, D])
    wgu = ext("ffn_w_gate_up", [DEPTH, 2, D, 2 * DFF])
    wdn = ext("ffn_w_down", [DEPTH, 2, DFF, D])
    ln_g = ext("ln_g", [DEPTH, 3, D])
    ln_b = ext("ln_b", [DEPTH, 3, D])
    sb_w_in = ext("sb_w_in", [1, D, 3 * D])
    sb_w_out = ext("sb_w_out", [1, D, D])
    ssd_w_in = ext("ssd_w_in", [1, D, SSD_IN])
    ssd_conv_w = ext("ssd_conv_w", [1, 4, SSD_CONV])
    ssd_conv_b = ext("ssd_conv_b", [1, SSD_CONV])
    ssd_dt_bias = ext("ssd_dt_bias", [1, SSD_H])
    ssd_a_log = ext("ssd_a_log", [1, SSD_H])
    ssd_d = ext("ssd_d", [1, SSD_H])
    ssd_norm_g = ext("ssd_norm_g", [1, SSD_DI])
    ssd_w_out = ext("ssd_w_out", [1, SSD_DI, D])
    c.consts = ext("consts", [128, 2752])
    c.cwl = ext("cwl", [128, 240])
    out = P.dram("out", [S, D], F32, kind="ExternalOutput").ap()
    XA = P.dram("XA", [S, D], F32).ap()
    Y = P.dram("Y", [S, D], F32).ap()
    c.HT = P.dram("HT", [TT, 128, DFF], BF16).ap()
    wpg = nc.sbuf_tensor("wpre", [128, 44 * 512], BF16)
    c.wpre = wpg.__enter__()
    xT_state = {"g": nc.sbuf_tensor("xT", [128, KC, S], BF16)}
    c.xT = xT_state["g"].__enter__()

    steps = []

    xT_n = [0]

    def xT_free():
        xT_state["g"].__exit__(None, None, None)
        xT_state["g"] = None

    def xT_alloc():
        if xT_state["g"] is None:
            xT_n[0] += 1
            xT_state["g"] = nc.sbuf_tensor("xT_%d" % xT_n[0], [128, KC, S], BF16)
            c.xT = xT_state["g"].__enter__()

    def ln(*a, **k):
        xT_alloc()
        ph_ln(P, c, *a, **k)

    def pf_gu(i, j):
        return lambda: gu_load(P, wview(c, "gu"), "wpre", wgu[i, j], 0)

    def pf_dn(wd, nk):
        return lambda: dn_load(P, wview(c, "dn", nk=nk), "wpre", wd, nk, 0)

    def pf_pj(w, col0, nb):
        return lambda: pj_load(P, wview(c, "pj", nb=nb), "wpre", w, col0, 0, nb)

    def ffn(i, j, xres, pre, nxt):
        ph_gate_up(P, c, wgu[i, j], pre=pre, nxt=pf_dn(wdn[i, j], DFF // 128))
        xT_free()
        ph_down(P, c, wdn[i, j], DFF // 128, c.HT, xres, Y, 0.5, pre=True, nxt=nxt)

    n = [0]

    def done():
        n[0] += 1
        return stop_after is not None and n[0] >= stop_after

    def finish(src):
        ph_copy(P, c, src, out)
        if xT_state["g"] is not None:
            xT_free()
        wpg.__exit__(None, None, None)
        P.close()
        return P

    c.QKT = P.dram("QKT", [2 * NH, 128, S], BF16).ap()
    c.QT = c.QKT[0:NH]
    c.KT = c.QKT[NH:2 * NH]
    c.V4 = P.dram("V4", [NH, 128, TT, 128], BF16).ap()
    c.OT4 = P.dram("OT4", [TT, 128, NH * 128], BF16).ap()
    XB = P.dram("XB", [S, D], F32).ap()

    def attn_mixer(xres, nxt):
        w = sb_w_in[0]
        ph_fm(P, c, w, 0, 2 * D, c.QKT, BF16, scale_of_chunk=lambda ch: (HD ** -0.5) if ch < NH else 1.0,
              pre=True, nxt=pf_pj(w, 2 * D, 512))
        V4v = c.V4.rearrange("h p tt d -> p tt h d")

        def store_v(e, tile, fb, tt):
            return e.dma_start(out=V4v[:, tt, fb * 4:(fb + 1) * 4, :], in_=tile[:].rearrange("p (h d) -> p h d", d=128))
        ph_tm(P, c, w, 2 * D, D, store_v, BF16, NS=2, pre=True, nxt=pf_dn(sb_w_out[0], NH))
        xT_free()
        ph_attn(P, c)
        ph_down(P, c, sb_w_out[0], NH, c.OT4, xres, Y, 1.0, pre=True, nxt=nxt)

    full = stop_after is None
    ln(x_in, None, None, None, do_ln=False)
    ffn(0, 0, x_in, False, pf_pj(sb_w_in[0], 0, 512) if full or stop_after >= 3 else None)
    if done():
        return finish(Y)
    ln(Y, XA, ln_g[0, 0:1, :], ln_b[0, 0:1, :])
    if done():
        return finish(XA)
    attn_mixer(XA, pf_gu(0, 1))
    ln(Y, XB, ln_g[0, 1:2, :], ln_b[0, 1:2, :])
    if done():
        return finish(XB)
    ffn(0, 1, XB, True, pf_gu(1, 0))
    ln(Y, XA, ln_g[0, 2:3, :], ln_b[0, 2:3, :])
    if done():
        return finish(XA)
    w = ssd_w_in[0]
    ffn(1, 0, XA, True, pf_pj(w, 0, 512))
    ln(Y, XB, ln_g[1, 0:1, :], ln_b[1, 0:1, :])
    if done():
        return finish(XB)
    c.ZS = P.dram("ZS", [S, SSD_DI], F32).ap()
    c.XBCT = P.dram("XBCT", [48, 128, S], F32).ap()
    c.DTR = P.dram("DTR", [S, SSD_H], F32).ap()
    c.XS = P.dram("XS", [S, SSD_DI], F32).ap()
    c.BT = P.dram("BT", [8, 128, S], BF16).ap()
    c.CT = P.dram("CT", [8, 128, S], BF16).ap()
    c.BTOK = P.dram("BTOK", [S, 1024], BF16).ap()
    c.SM = P.dram("SM", [4, S, SSD_H], F32).ap()
    c.YT4 = P.dram("YT4", [TT, 128, 32 * 128], BF16).ap()
    ph_tm(P, c, w, 0, SSD_DI, lambda e, tile, fb, tt: e.dma_start(
        out=c.ZS[tt * 128:(tt + 1) * 128, fb * 512:(fb + 1) * 512], in_=tile[:]), F32,
        pre=True, nxt=pf_pj(w, SSD_DI + SSD_CONV, SSD_H))
    ph_tm(P, c, w, SSD_DI + SSD_CONV, SSD_H, lambda e, tile, fb, tt: e.dma_start(
        out=c.DTR[tt * 128:(tt + 1) * 128, :], in_=tile[:]), F32, nb=64, NS=2, pre=True, nxt=pf_pj(w, SSD_DI, 512))
    ph_fm(P, c, w, SSD_DI, SSD_CONV, c.XBCT, F32, pre=True, nxt=pf_dn(ssd_w_out[0], SSD_DI // 128))
    xT_free()
    ph_ssd_conv(P, c)
    ph_ssd_dt(P, c, ssd_dt_bias[0:1, :], ssd_a_log[0:1, :])
    ph_ssd_main(P, c, ssd_d[0:1, :], ssd_norm_g[0:1, :])
    ph_down(P, c, ssd_w_out[0], SSD_DI // 128, c.YT4, XB, Y, 1.0, pre=True, nxt=pf_gu(1, 1))
    ln(Y, XA, ln_g[1, 1:2, :], ln_b[1, 1:2, :])
    if done():
        return finish(XA)
    ffn(1, 1, XA, True, None)
    ph_ln(P, c, Y, None, ln_g[1, 2:3, :], ln_b[1, 2:3, :], final_out=out)
    if xT_state["g"] is not None:
        xT_free()
    wpg.__exit__(None, None, None)
    P.close()
    return P


def ph_copy(P, c, src, dst):
    t = [P.sbuf("cp%d" % i, [128, D], F32) for i in range(2)]
    for tt in range(TT):
        i = tt % 2
        rows = slice(tt * 128, (tt + 1) * 128)
        P.op("sp", lambda e, i=i, rows=rows: e.dma_start(out=t[i][:], in_=src[rows, :]), writes=[("t", i)], dma=("t", i))
        P.op("sp", lambda e, i=i, rows=rows: e.dma_start(out=dst[rows, :], in_=t[i][:]), reads=[("t", i)], dma=("t", i))
    P.end_phase()


def make_consts():
    cst = np.zeros((128, 2752), np.float32)
    cst[:, 0:128] = np.eye(128, dtype=np.float32)
    j = np.arange(128)[:, None]
    t = np.arange(512)[None, :]
    for i in range(4):
        cst[:, 128 + i * 512:128 + (i + 1) * 512] = (128 * i + j < t)
    s_ = np.arange(128)[None, :]
    cst[:, 2176:2304] = -(j >= s_).astype(np.float32)
    cst[:, 2304:2432] = -1.0
    k = np.arange(128)[:, None]
    i_ = np.arange(128)[None, :]
    same = (k // 64) == (i_ // 64)
    cst[:, 2432:2560] = (same & (k <= i_))
    cst[:, 2560:2688] = same
    cst[:, 2688:2752] = ((k % 64) <= np.arange(64)[None, :])
    return cst


_CACHE = {}


def run(inputs, stop_after=None, n_cores=8):
    key = stop_after
    if key not in _CACHE:
        _CACHE[key] = build(stop_after)
    P = _CACHE[key]
    names = ["ffn_w_gate_up", "ffn_w_down", "ln_g", "ln_b", "sb_w_in", "sb_w_out", "ssd_w_in", "ssd_conv_w",
             "ssd_conv_b", "ssd_dt_bias", "ssd_a_log", "ssd_d", "ssd_norm_g", "ssd_w_out"]
    shared = {k: np.ascontiguousarray(np.asarray(inputs[k], dtype=np.float32)) for k in names}
    shared["consts"] = make_consts()
    cwb = np.concatenate([shared["ssd_conv_w"][0].T, shared["ssd_conv_b"][0][:, None]], axis=1)
    shared["cwl"] = np.ascontiguousarray(cwb.reshape(48, 128, 5).transpose(1, 0, 2).reshape(128, 240))
    x = np.asarray(inputs["x"], dtype=np.float32)
    in_maps = []
    for b in range(n_cores):
        m = dict(shared)
        m["x"] = np.ascontiguousarray(x[b])
        in_maps.append(m)
    res = run_bass_kernel_spmd(P.nc, in_maps, core_ids=list(range(n_cores)))
    return np.stack([np.asarray(r["out"]) for r in res.results], axis=0)


def kernel(**inputs):
    return run(inputs).astype(np.float32)
```
